# Optimizing a Trainium2 kernel written in Bass

```python
import jax, jax.numpy as jnp
from jax import lax
import numpy as np

D_MODEL = 1024
BATCH = 2
SEQ = 8192
DEPTH = 1

GM_WIDTH = D_MODEL
GM_GROUPS = 8
GM_GROUP_DIM = GM_WIDTH // GM_GROUPS
GM_CHUNK = 128
N_HEADS = 8
N_KV_HEADS = 2
Q_PER_KV = N_HEADS // N_KV_HEADS
HEAD_DIM = 128
ATTN_WIDTH = N_HEADS * HEAD_DIM
KV_WIDTH = N_KV_HEADS * HEAD_DIM
CMP_BLOCK = 32
CMP_STRIDE = 16
SEL_BLOCK = 64
N_SELECT = 16
N_LOCAL_FORCED = 2
WINDOW = 512
Q_BLOCK = 128
N_NSA_BRANCH = 3
N_MERGE_BRANCH = 2
D_FF = 4 * D_MODEL
SPLIT_SIZES = (2 * GM_WIDTH, ATTN_WIDTH, KV_WIDTH, KV_WIDTH, KV_WIDTH, KV_WIDTH, KV_WIDTH, KV_WIDTH,
               N_NSA_BRANCH * N_HEADS, N_MERGE_BRANCH * D_MODEL)
D_IN = sum(SPLIT_SIZES)
DEEPNORM_ALPHA = (2.0 * DEPTH) ** 0.25
DEEPNORM_BETA = (8.0 * DEPTH) ** -0.25
LN_EPS = 1e-5
NEG_INF = -1e30
FORCED_SCORE = 1e9

kernel_name = "gmlp_nsa_griffin_merge_deepnorm"


def layer_norm(x, g, b):
    xf = x.astype(jnp.float32)
    mu = jnp.mean(xf, axis=-1, keepdims=True)
    var = jnp.mean(jnp.square(xf - mu), axis=-1, keepdims=True)
    y = (xf - mu) * lax.rsqrt(var + LN_EPS) * g.astype(jnp.float32) + b.astype(jnp.float32)
    return y.astype(x.dtype)


def alibi_slopes():
    h = jnp.arange(1, N_HEADS + 1, dtype=jnp.float32)
    return (2.0 ** (-8.0 * h / N_HEADS)).reshape(N_KV_HEADS, Q_PER_KV)


def spatial_gating(z, ln_g, ln_b, w_s, b_s):
    bsz, seq, _ = z.shape
    u, v = jnp.split(jax.nn.gelu(z), 2, axis=-1)
    v = layer_norm(v, ln_g, ln_b).reshape(bsz, seq // GM_CHUNK, GM_CHUNK, GM_GROUPS, GM_GROUP_DIM)
    causal = jnp.tril(jnp.ones((GM_CHUNK, GM_CHUNK), dtype=w_s.dtype))
    v = jnp.einsum("gij,bnjgd->bnigd", w_s * causal, v) + b_s.T[None, None, :, :, None]
    return u * v.reshape(bsz, seq, GM_WIDTH)


def compress(x, pe, w1, w2):
    bsz, seq = x.shape[:2]
    ch = x.reshape(bsz, seq // CMP_STRIDE, CMP_STRIDE, N_KV_HEADS, HEAD_DIM)
    blk = jnp.concatenate([ch[:, :-1], ch[:, 1:]], axis=2) + pe[None, None, :, None, :]
    flat = blk.transpose(0, 1, 3, 2, 4).reshape(bsz, seq // CMP_STRIDE - 1, N_KV_HEADS, CMP_BLOCK * HEAD_DIM)
    return jax.nn.gelu(flat @ w1) @ w2


def nsa_attention(q, k_cmp, v_cmp, k_slc, v_slc, k_win, v_win, gates):
    bsz, seq = q.shape[:2]
    n_cmp = seq // CMP_STRIDE - 1
    n_sel = seq // SEL_BLOCK
    k_top = min(N_SELECT, n_sel)
    slopes = alibi_slopes()
    q = (q * HEAD_DIM ** -0.5).reshape(bsz, seq, N_KV_HEADS, Q_PER_KV, HEAD_DIM)
    gates = gates.reshape(bsz, seq, N_KV_HEADS, Q_PER_KV, N_NSA_BRANCH)
    cmp_idx = jnp.arange(n_cmp)
    cmp_end = cmp_idx * CMP_STRIDE + CMP_BLOCK - 1
    cmp_ctr = cmp_idx.astype(jnp.float32) * CMP_STRIDE + 0.5 * (CMP_BLOCK - 1)
    sel_idx = jnp.arange(n_sel)
    overlap = ((cmp_end[:, None] >= sel_idx[None, :] * SEL_BLOCK)
               & (cmp_idx[:, None] * CMP_STRIDE <= sel_idx[None, :] * SEL_BLOCK + SEL_BLOCK - 1)).astype(jnp.float32)
    ks_blocks = k_slc.reshape(bsz, n_sel, SEL_BLOCK, N_KV_HEADS, HEAD_DIM).transpose(0, 3, 1, 2, 4)
    vs_blocks = v_slc.reshape(bsz, n_sel, SEL_BLOCK, N_KV_HEADS, HEAD_DIM).transpose(0, 3, 1, 2, 4)
    pad = ((0, 0), (WINDOW, 0), (0, 0), (0, 0))
    kw_pad = jnp.pad(k_win, pad)
    vw_pad = jnp.pad(v_win, pad)
    gather = jax.vmap(jax.vmap(lambda blocks, ix: blocks[ix]))
    offs_sel = jnp.arange(SEL_BLOCK)
    offs_win = jnp.arange(WINDOW + Q_BLOCK)

    def query_block(qb):
        q0 = qb * Q_BLOCK
        qc = lax.dynamic_slice_in_dim(q, q0, Q_BLOCK, axis=1)
        gc = lax.dynamic_slice_in_dim(gates, q0, Q_BLOCK, axis=1)
        t = q0 + jnp.arange(Q_BLOCK)
        valid_c = cmp_end[None, :] <= t[:, None]
        dist_c = t[:, None].astype(jnp.float32) - cmp_ctr[None, :]
        s = jnp.einsum("bqhgd,bnhd->bhgqn", qc, k_cmp).astype(jnp.float32) \
            - slopes[None, :, :, None, None] * dist_c
        p_c = jnp.where(valid_c, jax.nn.softmax(jnp.where(valid_c, s, NEG_INF), axis=-1), 0.0)
        o_c = jnp.einsum("bhgqn,bnhd->bqhgd", p_c.astype(v_cmp.dtype), v_cmp)
        imp = jnp.einsum("bhgqn,nm->bhqm", p_c, overlap)
        lag = (t // SEL_BLOCK)[:, None] - sel_idx[None, :]
        forced = (sel_idx[None, :] == 0) | ((lag >= 0) & (lag < N_LOCAL_FORCED))
        score = jnp.where(forced, FORCED_SCORE, jnp.where(lag >= 0, imp, -1.0))
        _, idx = lax.top_k(score, k_top)
        ks = gather(ks_blocks, idx)
        vs = gather(vs_blocks, idx)
        dist_s = (t[None, None, :, None, None] - (idx[..., None] * SEL_BLOCK + offs_sel))[:, :, None]
        s = jnp.einsum("bqhgd,bhqnkd->bhgqnk", qc, ks).astype(jnp.float32) \
            - slopes[None, :, :, None, None, None] * dist_s.astype(jnp.float32)
        s = jnp.where(dist_s >= 0, s, NEG_INF)
        shp = s.shape
        p_s = jax.nn.softmax(s.reshape(shp[0], shp[1], shp[2], shp[3], -1), axis=-1).reshape(shp)
        o_s = jnp.einsum("bhgqnk,bhqnkd->bqhgd", p_s.astype(vs.dtype), vs)
        kw = lax.dynamic_slice_in_dim(kw_pad, q0, WINDOW + Q_BLOCK, axis=1)
        vw = lax.dynamic_slice_in_dim(vw_pad, q0, WINDOW + Q_BLOCK, axis=1)
        dist_w = t[:, None] - (q0 - WINDOW + offs_win)[None, :]
        valid_w = (dist_w >= 0) & (dist_w < WINDOW) & (dist_w <= t[:, None])
        s = jnp.einsum("bqhgd,bkhd->bhgqk", qc, kw).astype(jnp.float32) \
            - slopes[None, :, :, None, None] * dist_w.astype(jnp.float32)
        p_w = jax.nn.softmax(jnp.where(valid_w, s, NEG_INF), axis=-1)
        o_w = jnp.einsum("bhgqk,bkhd->bqhgd", p_w.astype(vw.dtype), vw)
        return gc[..., 0:1] * o_c + gc[..., 1:2] * o_s + gc[..., 2:3] * o_w

    out = lax.map(query_block, jnp.arange(seq // Q_BLOCK))
    return jnp.moveaxis(out, 0, 1).reshape(bsz, seq, ATTN_WIDTH)


def setup_inputs(seed: int = 0) -> dict:
    key = jax.random.key(seed)
    ks = jax.random.split(key, 24)

    def nrm(k, shape, scale):
        return jax.random.normal(k, shape, jnp.float32) * scale

    L = DEPTH
    fan_cmp = CMP_BLOCK * HEAD_DIM
    return {
        "x": nrm(ks[0], (BATCH, SEQ, D_MODEL), 1.0),
        "w_in": nrm(ks[1], (L, D_MODEL, D_IN), D_MODEL ** -0.5),
        "gm_ln_g": 1.0 + nrm(ks[2], (L, GM_WIDTH), 0.02),
        "gm_ln_b": nrm(ks[3], (L, GM_WIDTH), 0.02),
        "gm_w_s": nrm(ks[4], (L, GM_GROUPS, GM_CHUNK, GM_CHUNK), GM_CHUNK ** -0.5),
        "gm_b_s": 1.0 + nrm(ks[5], (L, GM_GROUPS, GM_CHUNK), 0.02),
        "cmp_pe_k": nrm(ks[6], (L, CMP_BLOCK, HEAD_DIM), 0.02),
        "cmp_w1_k": nrm(ks[7], (L, fan_cmp, HEAD_DIM), fan_cmp ** -0.5),
        "cmp_w2_k": nrm(ks[8], (L, HEAD_DIM, HEAD_DIM), HEAD_DIM ** -0.5),
        "cmp_pe_v": nrm(ks[9], (L, CMP_BLOCK, HEAD_DIM), 0.02),
        "cmp_w1_v": nrm(ks[10], (L, fan_cmp, HEAD_DIM), fan_cmp ** -0.5),
        "cmp_w2_v": nrm(ks[11], (L, HEAD_DIM, HEAD_DIM), HEAD_DIM ** -0.5),
        "w_proj_gm": nrm(ks[12], (L, GM_WIDTH, D_MODEL), GM_WIDTH ** -0.5 * DEEPNORM_BETA),
        "w_proj_nsa": nrm(ks[13], (L, ATTN_WIDTH, D_MODEL), ATTN_WIDTH ** -0.5 * DEEPNORM_BETA),
        "w_out": nrm(ks[14], (L, D_MODEL, D_MODEL), D_MODEL ** -0.5 * DEEPNORM_BETA),
        "ln1_g": 1.0 + nrm(ks[15], (L, D_MODEL), 0.02),
        "ln1_b": nrm(ks[16], (L, D_MODEL), 0.02),
        "w_ff1": nrm(ks[17], (L, D_MODEL, D_FF), D_MODEL ** -0.5),
        "w_ff2": nrm(ks[18], (L, D_FF, D_MODEL), D_FF ** -0.5 * DEEPNORM_BETA),
        "ln2_g": 1.0 + nrm(ks[19], (L, D_MODEL), 0.02),
        "ln2_b": nrm(ks[20], (L, D_MODEL), 0.02),
    }


def reference(x, w_in, gm_ln_g, gm_ln_b, gm_w_s, gm_b_s, cmp_pe_k, cmp_w1_k, cmp_w2_k,
              cmp_pe_v, cmp_w1_v, cmp_w2_v, w_proj_gm, w_proj_nsa, w_out,
              ln1_g, ln1_b, w_ff1, w_ff2, ln2_g, ln2_b):
    bsz, seq, _ = x.shape
    split_points = np.cumsum(SPLIT_SIZES)[:-1].tolist()
    kv_shape = (bsz, seq, N_KV_HEADS, HEAD_DIM)
    h = x
    for l in range(DEPTH):
        z = h @ w_in[l]
        (z_gm, z_q, z_kc, z_vc, z_ks, z_vs, z_kw, z_vw, z_g, z_m) = jnp.split(z, split_points, axis=-1)
        y_gm = spatial_gating(z_gm, gm_ln_g[l], gm_ln_b[l], gm_w_s[l], gm_b_s[l])
        k_cmp = compress(z_kc.reshape(kv_shape), cmp_pe_k[l], cmp_w1_k[l], cmp_w2_k[l])
        v_cmp = compress(z_vc.reshape(kv_shape), cmp_pe_v[l], cmp_w1_v[l], cmp_w2_v[l])
        y_nsa = nsa_attention(z_q, k_cmp, v_cmp, z_ks.reshape(kv_shape), z_vs.reshape(kv_shape),
                              z_kw.reshape(kv_shape), z_vw.reshape(kv_shape), jax.nn.sigmoid(z_g))
        mg = jax.nn.sigmoid(z_m).reshape(bsz, seq, N_MERGE_BRANCH, D_MODEL)
        mix = (mg[:, :, 0] * (y_gm @ w_proj_gm[l]) + mg[:, :, 1] * (y_nsa @ w_proj_nsa[l])) @ w_out[l]
        h = layer_norm(DEEPNORM_ALPHA * h + mix, ln1_g[l], ln1_b[l])
        f = jnp.square(jax.nn.relu(h @ w_ff1[l])) @ w_ff2[l]
        h = layer_norm(DEEPNORM_ALPHA * h + f, ln2_g[l], ln2_b[l])
    return h
```

```python
import os
from contextlib import ExitStack

import numpy as np
import concourse.bass as bass
import concourse.mybir as mybir
from concourse.bass_utils import run_bass_kernel_spmd

F32 = mybir.dt.float32
BF16 = mybir.dt.bfloat16
AF = mybir.ActivationFunctionType
ALU = mybir.AluOpType

ENGINES = ("tensor", "vector", "scalar", "gpsimd", "sync")
NEG = -32768.0
ALPHA = 2.0 ** 0.25
QSCALE = 128.0 ** -0.5
LN_EPS = 1e-5


class Op:
    __slots__ = ("eng", "fn", "deps", "marked", "count", "is_dma", "dma_key", "dma_count", "idx")

    def __init__(self, eng, fn):
        self.eng = eng
        self.fn = fn
        self.deps = []
        self.marked = False
        self.count = 0
        self.is_dma = False
        self.dma_key = None
        self.dma_count = 0


class Prog:
    def __init__(self, nc):
        self.nc = nc
        self.ops = {e: [] for e in ENGINES}
        self.last_w = {}
        self.readers = {}
        self.dma_counts = {}
        self.dma_last = {}
        self.all_ops = []
        self.pending = {e: [] for e in ENGINES}

    def _add(self, eng, fn, reads, writes, dma_key=None):
        op = Op(eng, fn)
        if dma_key is not None:
            op.is_dma = True
            op.dma_key = dma_key
            self.dma_counts[dma_key] = self.dma_counts.get(dma_key, 0) + 16
            op.dma_count = self.dma_counts[dma_key]
            self.dma_last[dma_key] = op
        deps = list(self.pending[eng])
        self.pending[eng] = []
        for k in reads:
            w = self.last_w.get(k)
            if w is not None:
                deps.append(w)
            if isinstance(k, str) and k.startswith("bk"):
                for r in self.readers.get(k, ()):
                    if r.eng != eng:
                        deps.append(r)
        for k in writes:
            w = self.last_w.get(k)
            if w is not None:
                deps.append(w)
            deps.extend(self.readers.get(k, ()))
        op.idx = len(self.all_ops)
        best = {}
        for d in deps:
            if d is op:
                continue
            if (not d.is_dma) and (not op.is_dma) and d.eng == "tensor" and eng == "tensor":
                continue
            k = ("d", d.dma_key) if d.is_dma else ("e", d.eng)
            o = best.get(k)
            if o is None or d.idx > o.idx:
                best[k] = d
        op.deps = list(best.values())
        for k in reads:
            self.readers.setdefault(k, []).append(op)
        for k in writes:
            self.last_w[k] = op
            self.readers[k] = []
        self.all_ops.append(op)
        self.ops[eng].append(op)
        return op

    def op(self, eng, fn, reads=(), writes=()):
        return self._add(eng, fn, list(reads), list(writes))

    def dma(self, eng, fn, reads=(), writes=(), key=None):
        return self._add(eng, fn, list(reads), list(writes), dma_key=key)

    def barrier(self):
        snap = []
        for e in ENGINES:
            for op in reversed(self.ops[e]):
                if not op.is_dma:
                    snap.append(op)
                    break
        snap.extend(self.dma_last.values())
        for e in ENGINES:
            self.pending[e] = list(snap)

    def emit(self, final_waits=()):
        nc = self.nc
        for op in self.all_ops:
            for d in op.deps:
                if not d.is_dma:
                    d.marked = True
        for e in ENGINES:
            c = 0
            for op in self.ops[e]:
                if op.marked and not op.is_dma:
                    c += 1
                    op.count = c
        if os.environ.get("MK_VERBOSE"):
            print("sem counts", {e: max([o.count for o in self.ops[e]] + [0]) for e in ENGINES}, {e: len(self.ops[e]) for e in ENGINES}, "dma", max(self.dma_counts.values()))
        with ExitStack() as st:
            esem = {e: st.enter_context(nc.semaphore("p_" + e)) for e in ENGINES}
            dsem = {k: st.enter_context(nc.semaphore("d_" + str(k))) for k in self.dma_counts}
            block = st.enter_context(nc.Block())

            def run(engname):
                def body(eng):
                    waited = {}
                    for op in self.ops[engname]:
                        need = {}
                        for d in op.deps:
                            if d.is_dma:
                                k = ("d", d.dma_key)
                                v = d.dma_count
                            else:
                                k = ("e", d.eng)
                                v = d.count
                            if v > need.get(k, 0):
                                need[k] = v
                        for k, v in need.items():
                            if waited.get(k, 0) >= v:
                                continue
                            waited[k] = v
                            eng.wait_ge(dsem[k[1]] if k[0] == "d" else esem[k[1]], v)
                        ins = op.fn(eng)
                        if op.is_dma:
                            ins.then_inc(dsem[op.dma_key], 16)
                        elif op.marked:
                            ins.then_inc(esem[engname], 1)
                    if engname == "sync":
                        for k in final_waits:
                            eng.wait_ge(dsem[k], self.dma_counts[k])
                return body

            block.tensor(run("tensor"))
            block.vector(run("vector"))
            block.scalar(run("scalar"))
            block.gpsimd(run("gpsimd"))
            block.sync(run("sync"))


class Arena:
    def __init__(self, ap, nbytes):
        self.ap = ap
        self.nbytes = nbytes
        self.off = 0

    def alloc(self, free_shape, dtype):
        n = 1
        for s in free_shape:
            n *= s
        size = n * (4 if dtype == F32 else 2)
        self.off = (self.off + 63) // 64 * 64
        assert self.off + size <= self.nbytes, ("arena overflow", self.off, size, self.nbytes)
        sl = self.ap[:, self.off // 2:(self.off + size) // 2]
        self.off += size
        v = sl.bitcast(F32) if dtype == F32 else sl
        if len(free_shape) == 2:
            v = v.rearrange("p (a b) -> p a b", a=free_shape[0])
        elif len(free_shape) == 3:
            v = v.rearrange("p (a b c) -> p a b c", a=free_shape[0], b=free_shape[1])
        elif len(free_shape) == 4:
            v = v.rearrange("p (a b c d) -> p a b c d", a=free_shape[0], b=free_shape[1], c=free_shape[2])
        return v


class _Stop(Exception):
    pass


def build_program(debug=False, stop=""):
    nc = bass.Bass("TRN2", target_bir_lowering=False)

    def din(name, shape):
        return nc.dram_tensor(name, list(shape), F32, kind="ExternalInput").ap()

    xT = din("xT", [1024, 8192])
    xTo = din("xTo", [1024, 2048])
    xo = din("xo", [2048, 1024])
    w_in = din("w_in", [1024, 6680])
    cw1k = din("cw1k", [4096, 128]); cw2k = din("cw2k", [128, 128]); cpeTk = din("cpeTk", [128, 32])
    cw1v = din("cw1v", [4096, 128]); cw2v = din("cw2v", [128, 128]); cpeTv = din("cpeTv", [128, 32])
    wpg = din("wpg", [1024, 1024]); wpn = din("wpn", [1024, 1024]); wout = din("wout", [1024, 1024])
    wff1 = din("wff1", [1024, 4096]); wff2 = din("wff2", [4096, 1024])
    lngT_d = din("lngT", [128, 8]); lnbT_d = din("lnbT", [128, 8])
    wsT_d = din("wsT", [128, 8, 128]); gbs_d = din("gbs", [1024])
    ln1g_d = din("ln1g", [1024]); ln1b_d = din("ln1b", [1024]); ln2g_d = din("ln2g", [1024]); ln2b_d = din("ln2b", [1024])
    csel_d = din("csel", [128, 4, 128]); wsel_d = din("wsel", [128, 8, 128]); maskc_d = din("maskc", [32, 128])
    Lm_d = din("Lm", [32, 4, 128]); ALt_d = din("ALt", [3, 33, 128]); alr_d = din("alr", [16, 3, 6, 2, 512]); alr2_d = din("alr2", [16, 2, 3, 2, 512])
    selA_d = din("selA", [16, 128, 128]); selB_d = din("selB", [16, 128, 128])
    E64_d = din("E64", [128, 32, 128]); vca_d = din("vca", [128, 4, 2, 258]); ident_d = din("identc", [128, 128])
    out_d = nc.dram_tensor("out", [2048, 1024], F32, kind="ExternalOutput").ap()
    ynsa_d = nc.dram_tensor("ynsa_scr", [128, 16, 8, 128], BF16).ap()
    h_d = nc.dram_tensor("h_scr", [2048, 1024], F32).ap()
    dbg_d = None
    if debug:
        dbg_d = nc.dram_tensor("dbg", [2048, 1024], F32, kind="ExternalOutput").ap()

    xTv = xT.rearrange("(c p) t -> p c t", p=128)
    xTov = xTo.rearrange("(c p) t -> p c t", p=128)
    w_inv = w_in.rearrange("(c p) n -> p c n", p=128)

    ARENA_BYTES = 204 * 1024
    with ExitStack() as st:
        arena_t = st.enter_context(nc.sbuf_tensor("arena", [128, ARENA_BYTES // 2], BF16))
        banks = [st.enter_context(nc.psum_tensor("bk%d" % k, [128, 512], F32)) for k in range(8)]
        AR = Arena(arena_t[:], ARENA_BYTES)
        P = Prog(nc)
        bk = lambda k: "bk%d" % k
        cnt = {"ev": 0}

        def mm(out, lhsT, rhs, start, stop, reads, writes):
            P.op("tensor", lambda e: e.matmul(out, lhsT=lhsT, rhs=rhs, start=start, stop=stop), reads, writes)

        def tr(out, in_, reads, writes):
            P.op("tensor", lambda e: e.transpose(out=out, in_=in_, identity=ident[:]), list(reads) + ["ident"], writes)

        def act(out, in_, func, reads, writes, bias=None, scale=None):
            kw = {}
            if bias is not None:
                kw["bias"] = bias
            if scale is not None:
                kw["scale"] = scale
            P.op("scalar", lambda e: e.activation(out=out, in_=in_, func=func, **kw), reads, writes)

        def cp(eng, out, in_, reads, writes):
            if eng == "scalar":
                P.op("scalar", lambda e: e.copy(out=out, in_=in_), reads, writes)
            else:
                P.op(eng, lambda e: e.tensor_copy(out=out, in_=in_), reads, writes)

        def evac(out, in_, reads, writes):
            cnt["ev"] += 1
            cp("scalar" if cnt["ev"] % 2 else "vector", out, in_, reads, writes)

        def ts(eng, out, in0, s1, s2, op0, op1, reads, writes):
            if op1 is None:
                P.op(eng, lambda e: e.tensor_scalar(out=out, in0=in0, scalar1=s1, scalar2=None, op0=op0), reads, writes)
            else:
                P.op(eng, lambda e: e.tensor_scalar(out=out, in0=in0, scalar1=s1, scalar2=s2, op0=op0, op1=op1), reads, writes)

        def tt(eng, out, in0, in1, op, reads, writes):
            P.op(eng, lambda e: e.tensor_tensor(out=out, in0=in0, in1=in1, op=op), reads, writes)

        def stt(out, in0, scalar, in1, op0, op1, reads, writes):
            P.op("vector", lambda e: e.scalar_tensor_tensor(out=out, in0=in0, scalar=scalar, in1=in1, op0=op0, op1=op1), reads, writes)

        def vop(eng, name, reads, writes, **kw):
            P.op(eng, lambda e: getattr(e, name)(**kw), reads, writes)

        def castdma(out, in_, writes, key):
            last = out.shape[-1]
            if len(out.shape) >= 3 and last > 1024:
                for c0 in range(0, last, 1024):
                    castdma1(out[..., c0:c0 + 1024], in_[..., c0:c0 + 1024], writes, key)
            else:
                castdma1(out, in_, writes, key)

        def castdma1(out, in_, writes, key):
            P.dma("gpsimd", lambda e: e.dma_start(out=out, in_=in_, max_dma_last_dim=4096), (), writes, key=key)

        def ldma(out, in_, writes, key, reads=()):
            P.dma("sync", lambda e: e.dma_start(out=out, in_=in_), reads, writes, key=key)

        ident = AR.alloc([128], BF16)
        castdma(ident[:], ident_d, ["ident"], "ident")
        k_cmpT = AR.alloc([2, 512], BF16)
        vca = AR.alloc([4, 2, 258], BF16)
        castdma(vca[:], vca_d, ["vca"], "vca")
        persist_mark = AR.off
        Wkv = AR.alloc([8, 1024], BF16)
        Wq = AR.alloc([8, 1024], BF16)
        Wg = AR.alloc([8, 24], BF16)
        E64 = AR.alloc([32, 128], BF16)
        csel = AR.alloc([4, 128], BF16)
        wsel = AR.alloc([8, 128], BF16)
        maskc = AR.alloc([128], BF16)
        Lm = AR.alloc([4, 128], BF16)
        ALt = AR.alloc([33, 128], BF16)
        a_mark = AR.off

        def load_phaseA_weights():
            castdma(Wkv[:, :, 0:256], w_inv[:, :, 3584:3840], ["Wkv"], "Wkv")
            castdma(Wkv[:, :, 256:512], w_inv[:, :, 4096:4352], ["Wkv"], "Wkv")
            castdma(Wkv[:, :, 512:768], w_inv[:, :, 3840:4096], ["Wkv"], "Wkv")
            castdma(Wkv[:, :, 768:1024], w_inv[:, :, 4352:4608], ["Wkv"], "Wkv")
            castdma(Wq[:], w_inv[:, :, 2048:3072], ["Wq"], "Wq")
            castdma(Wg[:], w_inv[:, :, 4608:4632], ["Wg"], "Wg")
            castdma(E64[:], E64_d, ["E64"], "E64")
            castdma(csel[:], csel_d, ["csel"], "csel")
            castdma(wsel[:], wsel_d, ["wsel"], "wsel")
            castdma(maskc[0:32, :], maskc_d, ["maskc"], "maskc")
            castdma(Lm[0:32], Lm_d, ["Lm"], "Lm")
            castdma(ALt[0:3], ALt_d, ["ALt"], "ALt")

        try:
            Wc = AR.alloc([8, 512], BF16)
            w1k = AR.alloc([32, 128], BF16); w1v = AR.alloc([32, 128], BF16)
            w2k = AR.alloc([128], BF16); w2v = AR.alloc([128], BF16)
            peTk = AR.alloc([32], BF16); peTv = AR.alloc([32], BF16)
            biasK = AR.alloc([1], F32); biasV = AR.alloc([1], F32)
            kc_all = AR.alloc([2, 8208], BF16); vc_all = AR.alloc([2, 8208], BF16)
            xg = AR.alloc([8, 512], BF16)
            hT = AR.alloc([2, 128], BF16)

            castdma(Wc[:], w_inv[:, :, 3072:3584], ["Wc"], "Wc")
            castdma(w1k[:], cw1k.rearrange("(p d) f -> d p f", d=128), ["w1k"], "w1k")
            castdma(w1v[:], cw1v.rearrange("(p d) f -> d p f", d=128), ["w1v"], "w1v")
            castdma(w2k[:], cw2k, ["w2k"], "w2k")
            castdma(w2v[:], cw2v, ["w2v"], "w2v")
            castdma(peTk[:], cpeTk, ["peTk"], "peTk")
            castdma(peTv[:], cpeTv, ["peTv"], "peTv")
            P.op("vector", lambda e: e.memset(kc_all[:, :, 0:16], 0.0), (), ["kc_all"])
            P.op("vector", lambda e: e.memset(vc_all[:, :, 0:16], 0.0), (), ["vc_all"])
            for (w1, peT, bias_t, nm, b) in ((w1k, peTk, biasK, "k", 6), (w1v, peTv, biasV, "v", 7)):
                for p in range(32):
                    mm(banks[b][:, 0:1], w1[:, p, :], peT[:, p:p + 1], p == 0, p == 31, ["w1" + nm, "peT" + nm], [bk(b)])
                cp("vector", bias_t[:], banks[b][:, 0:1], [bk(b)], ["bias" + nm])

            xg2 = AR.alloc([8, 512], BF16)
            for i in range(16):
                xg_ = (xg, xg2)[i % 2]
                xk = "xg%d" % (i % 2)
                castdma(xg_[:], xTv[:, :, 512 * i:512 * i + 512], [xk], xk)
                if i == 3:
                    load_phaseA_weights()
                for cg in range(4):
                    b = cg
                    for dc in range(8):
                        mm(banks[b][:, :], Wc[:, dc, 128 * cg:128 * cg + 128], xg_[:, dc, :], dc == 0, dc == 7, ["Wc", xk], [bk(b)])
                    dst = (kc_all if cg < 2 else vc_all)
                    evac(dst[:, cg % 2, 16 + 512 * i:16 + 512 * i + 512], banks[b][:, :], [bk(b)], ["kc_all" if cg < 2 else "vc_all"])
            for t in range(4):
                for isk in (True, False):
                    w1, w2, raw, bias_t, nm = (w1k, w2k, kc_all, biasK, "k") if isk else (w1v, w2v, vc_all, biasV, "v")
                    b = 4 if isk else 5
                    rawkey = "kc_all" if isk else "vc_all"
                    for p in range(32):
                        lo = p + 2048 * t
                        mm(banks[b][:, 0:256], w1[:, p, :], raw[:, :, lo:lo + 16 * 127 + 1:16], p == 0, p == 31, ["w1" + nm, rawkey], [bk(b)])
                    act(hT[:].rearrange("p a b -> p (a b)"), banks[b][:, 0:256], AF.Gelu_apprx_tanh, [bk(b), "bias" + nm], ["hT"], bias=bias_t[:, 0:1])
                    if isk:
                        mm(banks[6][:, 0:256], w2[:, :], hT[:].rearrange("p a b -> p (a b)"), True, True, ["w2k", "hT"], [bk(6)])
                        evac(k_cmpT[:, :, 128 * t:128 * t + 128], banks[6][:, 0:256].rearrange("p (a b) -> p a b", a=2), [bk(6)], ["k_cmpT"])
                    else:
                        for hk in range(2):
                            mm(banks[7][:, 128 * hk:128 * hk + 128], hT[:, hk, :], w2[:, :], True, True, ["w2v", "hT"], [bk(7)])
                        evac(vca[:, t, :, 0:128], banks[7][:, 0:256].rearrange("p (a b) -> p a b", a=2), [bk(7)], ["vca"])

            vop0 = vca[0:1, 0, :, 0:128]
            P.op("vector", lambda e: e.memset(vop0, 0.0), ["vca"], ["vca"])
            if stop == "A0":
                if debug:
                    dtmp = AR.alloc([3, 1024], F32)
                    cp("vector", dtmp[:, 0, :], k_cmpT[:].rearrange("p a b -> p (a b)"), ["k_cmpT"], ["dtmp"])
                    vflat = vca[:].rearrange("p a b c -> p (a b c)")
                    cp("vector", dtmp[:, 1, :], vflat[:, 0:1024], ["vca"], ["dtmp"])
                    cp("vector", dtmp[:, 2, :], vflat[:, 1024:2048], ["vca"], ["dtmp"])
                    for k3 in range(3):
                        ldma(dbg_d[128 * k3:128 * k3 + 128, :], dtmp[:, k3, :], [], "dbg", reads=["dtmp"])
                raise _Stop()
            P.barrier()
            AR.off = a_mark
            ksT = AR.alloc([2, 8192], BF16)
            vs_aug = AR.alloc([64, 2, 130], BF16)
            kwT = AR.alloc([2, 2, 512], BF16)
            vw_aug = AR.alloc([2, 4, 2, 130], BF16)
            xg = AR.alloc([8, 512], BF16)
            xq = AR.alloc([8, 128], BF16)
            qT = AR.alloc([8, 128], BF16)
            Rc = [AR.alloc([2, 512], BF16) for _ in range(2)]
            Rw = [AR.alloc([2, 512], BF16) for _ in range(2)]
            alr = [AR.alloc([6, 2, 512], BF16) for _ in range(2)]
            selA = [AR.alloc([128], F32) for _ in range(2)]
            selB = [AR.alloc([128], F32) for _ in range(2)]
            PT = [AR.alloc([4, 128], BF16) for _ in range(6)]
            y_acc = AR.alloc([8, 128], F32)
            ybf = AR.alloc([8, 128], BF16)
            ynT = AR.alloc([8, 128], BF16)
            gates = AR.alloc([24], F32)
            imp = AR.alloc([128], F32)
            sc = AR.alloc([128], F32)
            sc2 = AR.alloc([128], F32)
            m8a = AR.alloc([8], F32); m8b = AR.alloc([8], F32)
            selb = AR.alloc([128], BF16)
            selbT = AR.alloc([128], BF16)
            zz = AR.alloc([4], F32); rz = AR.alloc([4], F32); coef = AR.alloc([4], F32)

            castdma(xg[:], xTv[:, :, 0:512], ["xg"], "xgA")
            castdma(xq[:], xTov[:, :, 0:128], ["xq"], "xq")
            P.op("gpsimd", lambda e: e.memset(vs_aug[:], 1.0), (), ["vs_aug"])
            for k2 in range(2):
                vop("gpsimd", "memset", (), ["Rc%d" % k2], ap=Rc[k2][:], constant=0.0)
                vop("gpsimd", "memset", (), ["Rw%d" % k2], ap=Rw[k2][:], constant=0.0)
            P.op("gpsimd", lambda e: e.memset(vw_aug[:], 1.0), (), ["vw_aug"])

            if os.environ.get("MK_SUB", "") == "setup":
                raise _Stop()
            def bcast4(ap2d):
                return ap2d.unsqueeze(1).to_broadcast([ap2d.shape[0], 4, 128])

            def finalize_branch(hk, br, first):
                for g in range(4):
                    cp("vector", zz[:, g:g + 1], banks[2 + g][:, 128:129], [bk(2 + g)], ["zz"])
                ts("vector", zz[:], zz[:], 1e-30, None, ALU.max, None, ["zz"], ["zz"])
                P.op("vector", lambda e: e.reciprocal(out=rz[:], in_=zz[:]), ["zz"], ["rz"])
                gv = gates[:].rearrange("p (h r) -> p h r", r=3)[:, 4 * hk:4 * hk + 4, br]
                tt("vector", coef[:], rz[:], gv, ALU.mult, ["rz", "gates"], ["coef"])
                for g in range(4):
                    h = 4 * hk + g
                    if first:
                        ts("vector", y_acc[:, h, :], banks[2 + g][:, 0:128], coef[:, g:g + 1], None, ALU.mult, None,
                           [bk(2 + g), "coef"], ["y_acc"])
                    else:
                        stt(y_acc[:, h, :], banks[2 + g][:, 0:128], coef[:, g:g + 1], y_acc[:, h, :], ALU.mult, ALU.add,
                            [bk(2 + g), "coef", "y_acc"], ["y_acc"])

            sbank = [0, 1, 6, 7]
            scount = {"n": 0}

            def next_sbank():
                scount["n"] += 1
                return sbank[scount["n"] % 4]

            pcount = {"n": 0}

            def next_PT():
                pcount["n"] += 1
                k = pcount["n"] % 6
                return PT[k], "PT%d" % k

            NG = int(os.environ.get("MK_NG", "16"))
            for i in range(NG):
                par = i % 2
                ldma(selA[par][:], selA_d[i], ["selA%d" % par], "selA%d" % par)
                ldma(selB[par][:], selB_d[i], ["selB%d" % par], "selB%d" % par)
                castdma(alr[par][0:3], alr_d[i], ["alr%d" % par], "alr%d" % par)
                for cg in range(4):
                    b = (0, 1, 6, 7)[cg]
                    for dc in range(8):
                        mm(banks[b][:, :], Wkv[:, dc, 128 * cg:128 * cg + 128], xg[:, dc, :], dc == 0, dc == 7, ["Wkv", "xg"], [bk(b)])
                    if cg < 2:
                        evac(ksT[:, cg, 512 * i:512 * i + 512], banks[b][:, :], [bk(b)], ["ksT"])
                    else:
                        evac(kwT[:, par, cg - 2, :], banks[b][:, :], [bk(b)], ["kwT%d" % par])
                if os.environ.get("MK_SUB", "") == "kproj":
                    raise _Stop()
                for sub in range(4):
                    b = (0, 1, 6, 7)[sub]
                    for dc in range(8):
                        mm(banks[b][:, :], xg[:, dc, 128 * sub:128 * sub + 128], Wkv[:, dc, 512:1024], dc == 0, dc == 7, ["Wkv", "xg"], [bk(b)])
                    mkx = os.environ.get("MK_X", "")
                    if mkx not in ("1", "3"):
                        evac(vs_aug[:, 4 * i + sub, :, 0:128], banks[b][:, 0:256].rearrange("p (a b) -> p a b", a=2), [bk(b)], ["vs_aug"])
                    if mkx not in ("2", "3"):
                        evac(vw_aug[:, par, sub, :, 0:128], banks[b][:, 256:512].rearrange("p (a b) -> p a b", a=2), [bk(b)], ["vw_aug%d" % par])
                if i + 1 < NG:
                    castdma(xg[:], xTv[:, :, 512 * (i + 1):512 * (i + 1) + 512], ["xg"], "xgA")
                if os.environ.get("MK_SUB", "") == "vproj":
                    raise _Stop()
                for hb in range(2):
                    b = 6 + hb
                    for g in range(4):
                        h = 4 * hb + g
                        for dc in range(8):
                            mm(banks[b][:, 128 * g:128 * g + 128], Wq[:, dc, 128 * h:128 * h + 128], xq[:, dc, :], dc == 0, dc == 7, ["Wq", "xq"], [bk(b)])
                    act(qT[:, 4 * hb:4 * hb + 4, :].rearrange("p a b -> p (a b)"), banks[b][:, :], AF.Copy, [bk(b)], ["qT"], scale=QSCALE)
                if os.environ.get("MK_SUB", "") == "qproj":
                    raise _Stop()
                for dc in range(8):
                    mm(banks[0][:, 0:24], xq[:, dc, :], Wg[:, dc, :], dc == 0, dc == 7, ["Wg", "xq"], [bk(0)])
                act(gates[:], banks[0][:, 0:24], AF.Sigmoid, [bk(0)], ["gates"])
                if i + 1 < NG:
                    castdma(xq[:], xTov[:, :, 128 * (i + 1):128 * (i + 1) + 128], ["xq"], "xq")

                if os.environ.get("MK_SUB", "") == "proj":
                    raise _Stop()
                for hk in range(2):
                    q4 = qT[:, 4 * hk:4 * hk + 4, :].rearrange("p a b -> p (a b)")
                    units = []
                    p2 = (2 * i + hk) % 2
                    castdma(Rc[p2][64:67, :, :], alr2_d[i, hk], ["Rc%d" % p2], "Rc%d" % p2)
                    castdma(Rw[p2][64:67, :, :], alr2_d[i, hk], ["Rw%d" % p2], "Rw%d" % p2)

                    nt = i // 4 + 1

                    def cmp_S(t, last, M):
                        sb_ = next_sbank()
                        mm(banks[sb_][0:M, :], k_cmpT[:, hk, 128 * t:128 * t + M], q4, True, False, ["k_cmpT", "qT"], [bk(sb_)])
                        mm(banks[sb_][0:M, :], ALt[0:3, 32, 0:M], alr[par][0:3, 2 + t, hk, :], False, not last, ["ALt", "alr%d" % par], [bk(sb_)])
                        if last:
                            mm(banks[sb_][0:M, :], Lm[0:32, i % 4, 0:M], bcast4(maskc[0:32, :]), False, True, ["Lm", "maskc"], [bk(sb_)])
                        pt, ptk = next_PT()
                        act(pt[0:M, :, :].rearrange("p a b -> p (a b)"), banks[sb_][0:M, :], AF.Exp, [bk(sb_)], [ptk])
                        return pt, ptk

                    def cmp_PV(t, last, M, pt, ptk):
                        for g in range(4):
                            mm(banks[2 + g][:, 0:257], pt[0:M, g, :], vca[0:M, t, hk, 0:257], t == 0, last, [ptk, "vca"], [bk(2 + g)])
                        if last:
                            finalize_branch(hk, 0, True)
                            for g in range(4):
                                if g == 0:
                                    ts("vector", imp[:], banks[2][:, 129:257], rz[:, 0:1], None, ALU.mult, None, [bk(2), "rz"], ["imp"])
                                else:
                                    stt(imp[:], banks[2 + g][:, 129:257], rz[:, g:g + 1], imp[:], ALU.mult, ALU.add, [bk(2 + g), "rz", "imp"], ["imp"])
                            tt("vector", sc[:], imp[:], selA[par][:], ALU.mult, ["imp", "selA%d" % par], ["sc"])
                            tt("vector", sc[:], sc[:], selB[par][:], ALU.add, ["sc", "selB%d" % par], ["sc"])
                            P.op("vector", lambda e: e.max(out=m8a[:], in_=sc[:]), ["sc"], ["m8a"])
                            P.op("vector", lambda e: e.match_replace(out=sc2[:], in_to_replace=m8a[:], in_values=sc[:], imm_value=-1e30), ["sc", "m8a"], ["sc2"])
                            P.op("vector", lambda e: e.max(out=m8b[:], in_=sc2[:]), ["sc2"], ["m8b"])
                            ts("vector", selb[:], sc[:], m8b[:, 7:8], NEG, ALU.is_lt, ALU.mult, ["sc", "m8b"], ["selb"])
                            tb = banks[7][:].bitcast(BF16)
                            tr(tb[0:64, 0:128], selb[:, 0:64], ["selb"], [bk(7)])
                            tr(tb[0:64, 128:256], selb[:, 64:128], ["selb"], [bk(7)])
                            for a_ in range(2):
                                cp("vector", Rc[p2][0:64, a_, :].rearrange("p (g q) -> p g q", g=4),
                                   tb[0:64, 128 * a_:128 * a_ + 128].unsqueeze(1).to_broadcast([64, 4, 128]), [bk(7)], ["Rc%d" % p2])

                    for t in range(nt):
                        last = (t == nt - 1)
                        M = 32 * (i % 4 + 1) if last else 128
                        units.append((lambda t=t, last=last, M=M: cmp_S(t, last, M),
                                      lambda pt, ptk, t=t, last=last, M=M: cmp_PV(t, last, M, pt, ptk)))

                    r8s = list(range(8)) if i > 0 else list(range(4, 8))

                    def win_S(r8):
                        wpar = (i - 1) % 2 if r8 < 4 else par
                        sub = r8 % 4
                        kt = 4 * (i - 1) + r8
                        sb_ = next_sbank()
                        mm(banks[sb_][:, :], kwT[:, wpar, hk, 128 * sub:128 * sub + 128], q4, True, False, ["kwT%d" % wpar, "qT"], [bk(sb_)])
                        mm(banks[sb_][:, :], E64[:, kt % 32, :], Rw[p2][:, kt // 32, :], False, False, ["E64", "Rw%d" % p2], [bk(sb_)])
                        mm(banks[sb_][:, :], ident[:], bcast4(wsel[:, r8, :]), False, True, ["ident", "wsel"], [bk(sb_)])
                        pt, ptk = next_PT()
                        act(pt[:].rearrange("p a b -> p (a b)"), banks[sb_][:, :], AF.Exp, [bk(sb_)], [ptk])
                        return pt, ptk

                    def win_PV(r8, first, last, pt, ptk):
                        wpar = (i - 1) % 2 if r8 < 4 else par
                        sub = r8 % 4
                        for g in range(4):
                            mm(banks[2 + g][:, 0:129], pt[:, g, :], vw_aug[:, wpar, sub, hk, 0:129], first, last,
                               [ptk, "vw_aug%d" % wpar], [bk(2 + g)])
                        if last:
                            finalize_branch(hk, 2, False)

                    for idx, r8 in enumerate(r8s):
                        units.append((lambda r8=r8: win_S(r8),
                                      lambda pt, ptk, r8=r8, f=(idx == 0), l=(idx == len(r8s) - 1): win_PV(r8, f, l, pt, ptk)))

                    nk = 4 * i + 4

                    def sel_S(kt):
                        diag = kt >= 4 * i
                        sb_ = next_sbank()
                        mm(banks[sb_][:, :], ksT[:, hk, 128 * kt:128 * kt + 128], q4, True, False, ["ksT", "qT"], [bk(sb_)])
                        mm(banks[sb_][:, :], E64[:, kt % 32, :], Rc[p2][:, kt // 32, :], False, not diag, ["E64", "Rc%d" % p2], [bk(sb_)])
                        if diag:
                            mm(banks[sb_][:, :], ident[:], bcast4(csel[:, kt - 4 * i, :]), False, True, ["ident", "csel"], [bk(sb_)])
                        pt, ptk = next_PT()
                        act(pt[:].rearrange("p a b -> p (a b)"), banks[sb_][:, :], AF.Exp, [bk(sb_)], [ptk])
                        return pt, ptk

                    def sel_PV(kt, pt, ptk):
                        for g in range(4):
                            mm(banks[2 + g][:, 0:129], pt[:, g, :], vs_aug[:, kt, hk, 0:129], kt == 0, kt == nk - 1, [ptk, "vs_aug"], [bk(2 + g)])
                        if kt == nk - 1:
                            finalize_branch(hk, 1, False)

                    for kt in range(nk):
                        units.append((lambda kt=kt: sel_S(kt), lambda pt, ptk, kt=kt: sel_PV(kt, pt, ptk)))

                    LA = 3
                    pend = []
                    for (fS, fPV) in units:
                        pend.append((fPV, fS()))
                        if len(pend) > LA:
                            f_, r_ = pend.pop(0)
                            f_(*r_)
                    for f_, r_ in pend:
                        f_(*r_)
                if os.environ.get("MK_SUB", "") == "win":
                    raise _Stop()
                if debug:
                    ldma(dbg_d[128 * i:128 * i + 128, :], y_acc[:].rearrange("p a b -> p (a b)"), [], "dbg", reads=["y_acc"])
                cp("scalar", ybf[:], y_acc[:], ["y_acc"], ["ybf"])
                tb = banks[7][:].bitcast(BF16)
                for c in range(8):
                    tr(tb[:, 128 * c:128 * c + 128], ybf[:, c, :], ["ybf"], [bk(7)])
                cp("vector", ynT[:].rearrange("p a b -> p (a b)"), tb[:, :], [bk(7)], ["ynT"])
                ldma(ynsa_d[:, i, :, :], ynT[:], ["ynsa_d"], "ynsa_w", reads=["ynT"])

            if stop == "A":
                raise _Stop()
            P.barrier()
            AR.off = persist_mark
            ygT_all = AR.alloc([8, 2048], BF16)
            b1a_mark = AR.off
            Wu = AR.alloc([8, 1024], BF16)
            Wv = AR.alloc([8, 1024], BF16)
            xt = AR.alloc([8, 512], BF16)
            uT = AR.alloc([8, 512], BF16)
            vg = [AR.alloc([1024], F32) for _ in range(2)]
            vn = [AR.alloc([1024], BF16) for _ in range(2)]
            WcT = AR.alloc([8, 128], BF16)
            wsf = AR.alloc([8, 128], F32)
            Badd = AR.alloc([8, 128], F32)
            bsrep = AR.alloc([8, 128], F32)
            lngT = AR.alloc([8], F32); lnbT = AR.alloc([8], F32)
            ones = AR.alloc([128], BF16)
            stat = AR.alloc([2, 6], F32); mv = AR.alloc([2], F32); rstd = AR.alloc([1], F32)
            t1 = [AR.alloc([128], F32) for _ in range(2)]

            castdma(Wu[:], w_inv[:, :, 0:1024], ["Wu"], "Wu")
            castdma(Wv[:], w_inv[:, :, 1024:2048], ["Wv"], "Wv")
            ldma(wsf[:], wsT_d, ["wsf"], "wsf")
            ldma(bsrep[:].rearrange("p a b -> p (a b)"), gbs_d.partition_broadcast(128), ["bsrep"], "bsrep")
            ldma(lngT[:], lngT_d, ["lngT"], "lngT")
            ldma(lnbT[:], lnbT_d, ["lnbT"], "lnbT")
            P.op("vector", lambda e: e.memset(ones[:], 1.0), (), ["ones"])
            P.op("gpsimd", lambda e: e.affine_select(out=wsf[:], in_=wsf[:], pattern=[[0, 8], [1, 128]], compare_op=ALU.is_ge,
                                                     fill=0.0, base=0, channel_multiplier=-1), ["wsf"], ["wsf"])
            cp("vector", WcT[:], wsf[:], ["wsf"], ["WcT"])
            for g in range(8):
                b = g % 2
                mm(banks[b][:, 0:128], ones[:], WcT[:, g, :], True, True, ["ones", "WcT"], [bk(b)])
                stt(Badd[:, g, :], banks[b][:, 0:128], lnbT[:, g:g + 1], bsrep[:, g, :], ALU.mult, ALU.add,
                    [bk(b), "lnbT", "bsrep"], ["Badd"])

            def layer_norm_stats(src, srckey, stat, mv, rstd):
                vop("vector", "bn_stats", [srckey], ["stat"], out=stat[:, 0, :], in_=src[:, 0:512])
                vop("vector", "bn_stats", [srckey, "stat"], ["stat"], out=stat[:, 1, :], in_=src[:, 512:1024])
                vop("vector", "bn_aggr", ["stat"], ["mv"], out=mv[:], in_=stat[:].rearrange("p a b -> p (a b)"))
                act(rstd[:], mv[:, 1:2], AF.Sqrt, ["mv"], ["rstd"], bias=LN_EPS, scale=1.0)
                vop("vector", "reciprocal", ["rstd"], ["rstd"], out=rstd[:], in_=rstd[:])

            xtB = AR.alloc([8, 512], BF16)
            xtA = xt
            uTB = AR.alloc([8, 512], BF16)
            uTA = uT
            statB = AR.alloc([2, 6], F32); mvB = AR.alloc([2], F32); rstdB = AR.alloc([1], F32)

            def b1a_front(T, sub):
                xt_ = (xtA, xtB)[T % 2]
                xtk = "xt%d" % (T % 2)
                uT_ = (uTA, uTB)[T % 2]
                uk = "uT%d" % (T % 2)
                if sub == 0:
                    castdma(xt_[:], xTov[:, :, 512 * T:512 * T + 512], [xtk], xtk)
                    for cc in range(8):
                        b = cc % 4
                        for dc in range(8):
                            mm(banks[b][:, :], Wu[:, dc, 128 * cc:128 * cc + 128], xt_[:, dc, :], dc == 0, dc == 7, ["Wu", xtk], [bk(b)])
                        act(uT_[:, cc, :], banks[b][:, :], AF.Gelu_apprx_tanh, [bk(b)], [uk])
                v_ = vg[sub % 2]; vk = "vg%d" % (sub % 2)
                n_ = vn[sub % 2]; nk_ = "vn%d" % (sub % 2)
                st_, mv_, rs_ = ((stat, mv, rstd), (statB, mvB, rstdB))[sub % 2]
                sfx = "_%d" % (sub % 2)
                for half in range(2):
                    b = 4 + half
                    for dc in range(8):
                        mm(banks[b][:, :], xt_[:, dc, 128 * sub:128 * sub + 128], Wv[:, dc, 512 * half:512 * half + 512], dc == 0, dc == 7,
                           ["Wv", xtk], [bk(b)])
                    act(v_[:, 512 * half:512 * half + 512], banks[b][:, :], AF.Gelu_apprx_tanh, [bk(b)], [vk])
                vop("vector", "bn_stats", [vk], ["stat" + sfx], out=st_[:, 0, :], in_=v_[:, 0:512])
                vop("vector", "bn_stats", [vk, "stat" + sfx], ["stat" + sfx], out=st_[:, 1, :], in_=v_[:, 512:1024])
                vop("vector", "bn_aggr", ["stat" + sfx], ["mv" + sfx], out=mv_[:], in_=st_[:].rearrange("p a b -> p (a b)"))
                act(rs_[:], mv_[:, 1:2], AF.Sqrt, ["mv" + sfx], ["rstd" + sfx], bias=LN_EPS, scale=1.0)
                vop("vector", "reciprocal", ["rstd" + sfx], ["rstd" + sfx], out=rs_[:], in_=rs_[:])
                ts("vector", n_[:], v_[:], mv_[:, 0:1], rs_[:, 0:1], ALU.subtract, ALU.mult, [vk, "mv" + sfx, "rstd" + sfx], [nk_])

            def b1a_back(T, sub):
                uT_ = (uTA, uTB)[T % 2]
                uk = "uT%d" % (T % 2)
                n_ = vn[sub % 2]; nk_ = "vn%d" % (sub % 2)
                for g in range(8):
                    b = 6 + (g // 4) % 2
                    mm(banks[b][:, 128 * (g % 4):128 * (g % 4) + 128], n_[:, 128 * g:128 * g + 128], WcT[:, g, :], True, True,
                       [nk_, "WcT"], [bk(b)])
                    if g % 4 == 3:
                        for g2 in range(g - 3, g + 1):
                            tk = t1[g2 % 2]; tkk = "t1%d" % (g2 % 2)
                            stt(tk[:], banks[b][:, 128 * (g2 % 4):128 * (g2 % 4) + 128], lngT[:, g2:g2 + 1], Badd[:, g2, :], ALU.mult, ALU.add,
                                [bk(b), "lngT", "Badd"], [tkk])
                            tt("gpsimd", ygT_all[:, g2, 512 * T + 128 * sub:512 * T + 128 * sub + 128], tk[:], uT_[:, g2, 128 * sub:128 * sub + 128],
                               ALU.mult, [tkk, uk], ["ygT_all"])

            seq = [(T, sub) for T in range(4) for sub in range(4)]
            for k_, (T, sub) in enumerate(seq):
                b1a_front(T, sub)
                if k_ >= 1:
                    b1a_back(*seq[k_ - 1])
            b1a_back(*seq[-1])

            if stop == "B1a":
                raise _Stop()
            P.barrier()
            AR.off = b1a_mark
            Wm = AR.alloc([8, 2048], BF16)
            Wpg = AR.alloc([8, 1024], BF16); Wpn = AR.alloc([8, 1024], BF16); Wo = AR.alloc([8, 1024], BF16)
            xt = AR.alloc([8, 512], BF16)
            ynt = AR.alloc([8, 4, 128], BF16)
            sg = [AR.alloc([512], F32) for _ in range(2)]
            m01 = [AR.alloc([512], F32) for _ in range(2)]
            mrgT = AR.alloc([8, 512], BF16)
            xot = AR.alloc([1024], F32)
            r1 = AR.alloc([1024], F32)
            lg = AR.alloc([1024], F32); lb = AR.alloc([1024], F32)
            stat = AR.alloc([2, 6], F32); mv = AR.alloc([2], F32); rstd = AR.alloc([1], F32)

            castdma(Wpg[:], wpg.rearrange("(c p) n -> p c n", p=128), ["Wpg"], "Wpg")
            castdma(Wpn[:], wpn.rearrange("(c p) n -> p c n", p=128), ["Wpn"], "Wpn")
            castdma(Wm[:], w_inv[:, :, 4632:6680], ["Wm"], "Wm")
            castdma(Wo[:], wout.rearrange("(c p) n -> p c n", p=128), ["Wo"], "Wo")
            ldma(lg[:], ln1g_d.partition_broadcast(128), ["lg"], "lg")
            ldma(lb[:], ln1b_d.partition_broadcast(128), ["lb"], "lb")

            def ln_tail(src, srckey, dst_dram, tag, stat, mv, rstd, lg, lb):
                layer_norm_stats(src, srckey, stat, mv, rstd)
                ts("vector", src[:], src[:], mv[:, 0:1], rstd[:, 0:1], ALU.subtract, ALU.mult, [srckey, "mv", "rstd"], [srckey])
                tt("gpsimd", src[:], src[:], lg[:], ALU.mult, [srckey, "lg"], [srckey])
                tt("gpsimd", src[:], src[:], lb[:], ALU.add, [srckey, "lb"], [srckey])
                ldma(dst_dram, src[:], [tag + "_dram"], tag, reads=[srckey])

            xtB = AR.alloc([8, 512], BF16)
            xtA = xt
            yntB = AR.alloc([8, 4, 128], BF16)
            yntA = ynt
            r1B = AR.alloc([1024], F32)
            r1A = r1
            for T in range(4):
                xt = (xtA, xtB)[T % 2]
                xtk = "xtb%d" % (T % 2)
                ynt = (yntA, yntB)[T % 2]
                yntk = "ynt%d" % (T % 2)
                castdma(xt[:], xTov[:, :, 512 * T:512 * T + 512], [xtk], xtk)
                for il in range(4):
                    ldma(ynt[:, :, il, :], ynsa_d[:, 4 * T + il, :, :], [yntk], yntk, reads=["ynsa_d"])
                for Dc in range(8):
                    s4 = 4 * (Dc % 2)
                    for dc in range(8):
                        mm(banks[s4 + 0][:, :], Wpg[:, dc, 128 * Dc:128 * Dc + 128], ygT_all[:, dc, 512 * T:512 * T + 512], dc == 0, dc == 7,
                           ["Wpg", "ygT_all"], [bk(s4 + 0)])
                    for dc in range(8):
                        mm(banks[s4 + 1][:, :], Wpn[:, dc, 128 * Dc:128 * Dc + 128], ynt[:, dc, :, :].rearrange("p a b -> p (a b)"), dc == 0, dc == 7,
                           ["Wpn", yntk], [bk(s4 + 1)])
                    for br in range(2):
                        for dc in range(8):
                            mm(banks[s4 + 2 + br][:, :], Wm[:, dc, 1024 * br + 128 * Dc:1024 * br + 128 * Dc + 128], xt[:, dc, :], dc == 0, dc == 7,
                               ["Wm", xtk], [bk(s4 + 2 + br)])
                        act(sg[br][:], banks[s4 + 2 + br][:, :], AF.Sigmoid, [bk(s4 + 2 + br)], ["sg%d" % br])
                    tt("vector", m01[0][:], sg[0][:], banks[s4 + 0][:, :], ALU.mult, ["sg0", bk(s4 + 0)], ["m0"])
                    tt("vector", m01[1][:], sg[1][:], banks[s4 + 1][:, :], ALU.mult, ["sg1", bk(s4 + 1)], ["m1"])
                    tt("gpsimd", mrgT[:, Dc, :], m01[0][:], m01[1][:], ALU.add, ["m0", "m1"], ["mrgT"])
                for sub in range(4):
                    o0 = 512 * T + 128 * sub
                    r1 = (r1A, r1B)[sub % 2]
                    r1k = "r1%d" % (sub % 2)
                    ldma(xot[:], xo[o0:o0 + 128, :], ["xot"], "xot")
                    for half in range(2):
                        b = half
                        for Dc in range(8):
                            mm(banks[b][:, :], mrgT[:, Dc, 128 * sub:128 * sub + 128], Wo[:, Dc, 512 * half:512 * half + 512], Dc == 0, Dc == 7,
                               ["mrgT", "Wo"], [bk(b)])
                        stt(r1[:, 512 * half:512 * half + 512], xot[:, 512 * half:512 * half + 512], ALPHA, banks[b][:, :], ALU.mult, ALU.add,
                            ["xot", bk(b)], [r1k])
                    ln_tail(r1, r1k, h_d[o0:o0 + 128, :], "h_w", stat, mv, rstd, lg, lb)

            if stop == "B1b":
                raise _Stop()
            P.barrier()
            AR.off = persist_mark
            W1 = AR.alloc([8, 4096], BF16)
            W2 = AR.alloc([32, 1024], BF16)
            hin = [AR.alloc([1024], F32) for _ in range(2)]
            hbf = AR.alloc([1024], BF16)
            hT2 = AR.alloc([8, 256], BF16)
            aT = AR.alloc([32, 256], BF16)
            r2 = AR.alloc([1024], F32)
            r2B = AR.alloc([1024], F32)
            r2A = r2
            rta = AR.alloc([256], F32); rtv = AR.alloc([256], F32)
            lg = AR.alloc([1024], F32); lb = AR.alloc([1024], F32)
            stat = AR.alloc([2, 6], F32); mv = AR.alloc([2], F32); rstd = AR.alloc([1], F32)
            w1v_ = wff1.rearrange("(c p) n -> p c n", p=128)
            castdma(W1[:, :, 0:2048], w1v_[:, :, 0:2048], ["W1"], "W1")
            castdma(W1[:, :, 2048:4096], w1v_[:, :, 2048:4096], ["W1"], "W1")
            w2v_ = wff2.rearrange("(c p) n -> p c n", p=128)
            for q in range(4):
                castdma(W2[:, 8 * q:8 * q + 8, :], w2v_[:, 8 * q:8 * q + 8, :], ["W2"], "W2")
            ldma(lg[:], ln2g_d.partition_broadcast(128), ["lg"], "lg2")
            ldma(lb[:], ln2b_d.partition_broadcast(128), ["lb"], "lb2")

            for T8 in range(8):
                for s2 in range(2):
                    o0 = 256 * T8 + 128 * s2
                    ldma(hin[s2][:], h_d[o0:o0 + 128, :], ["hin%d" % s2], "hin%d" % s2, reads=["h_w_dram"])
                    cp("scalar", hbf[:], hin[s2][:], ["hin%d" % s2], ["hbf"])
                    tb = banks[6 + s2][:].bitcast(BF16)
                    for c in range(8):
                        tr(tb[:, 128 * c:128 * c + 128], hbf[:, 128 * c:128 * c + 128], ["hbf"], [bk(6 + s2)])
                    cp("vector", hT2[:, :, 128 * s2:128 * s2 + 128], tb[:, :].rearrange("p (a b) -> p a b", a=8), [bk(6 + s2)], ["hT2"])
                for fc in range(32):
                    b = fc % 4
                    for dc in range(8):
                        mm(banks[b][:, 0:256], W1[:, dc, 128 * fc:128 * fc + 128], hT2[:, dc, :], dc == 0, dc == 7, ["W1", "hT2"], [bk(b)])
                    if fc % 2 == 0:
                        act(rta[:], banks[b][:, 0:256], AF.Relu, [bk(b)], ["rta"])
                        tt("gpsimd", aT[:, fc, :], rta[:], rta[:], ALU.mult, ["rta"], ["aT"])
                    else:
                        ts("vector", rtv[:], banks[b][:, 0:256], 0.0, None, ALU.max, None, [bk(b)], ["rtv"])
                        tt("vector", aT[:, fc, :], rtv[:], rtv[:], ALU.mult, ["rtv"], ["aT"])
                for s2 in range(2):
                    o0 = 256 * T8 + 128 * s2
                    r2 = (r2A, r2B)[s2]
                    r2k = "r2%d" % s2
                    for half in range(2):
                        b = 4 + half
                        for fc in range(32):
                            mm(banks[b][:, :], aT[:, fc, 128 * s2:128 * s2 + 128], W2[:, fc, 512 * half:512 * half + 512], fc == 0, fc == 31,
                               ["aT", "W2"], [bk(b)])
                        stt(r2[:, 512 * half:512 * half + 512], hin[s2][:, 512 * half:512 * half + 512], ALPHA, banks[b][:, :], ALU.mult, ALU.add,
                            ["hin%d" % s2, bk(b)], [r2k])
                    ln_tail(r2, r2k, out_d[o0:o0 + 128, :], "out_w", stat, mv, rstd, lg, lb)


        except _Stop:
            pass
        fin = list(P.dma_counts.keys())
        P.emit(final_waits=fin)
    return nc


def _core_constants(j):
    f = np.float32
    q = np.arange(128)
    key = np.arange(128)
    slopes = (2.0 ** (-(np.arange(8) + 1.0))).astype(np.float64)
    c = {}
    kp = 128 * np.arange(4)[None, :, None] + key[:, None, None]
    qp = 128 * j + q[None, None, :]
    c["csel"] = np.where(kp <= qp, 0.0, NEG).astype(f)
    kp8 = 128 * (np.arange(8)[None, :, None] - 4) + key[:, None, None]
    dist = qp - kp8
    c["wsel"] = np.where((dist >= 0) & (dist < 512), 0.0, NEG).astype(f)
    l = np.arange(32)
    c["maskc"] = np.where(16 * l[:, None] + 15 <= 128 * j + q[None, :], 0.0, NEG).astype(f)
    Lm = np.zeros((32, 4, 128), f)
    for r in range(4):
        Lm[l, r, 32 * r + l] = 1.0
    c["Lm"] = Lm
    p = np.arange(128)
    ALt = np.zeros((3, 33, 128), f)
    for b_ in range(32):
        ALt[0, b_, :] = p
        ALt[1, b_, :] = 128.0 * b_
        ALt[2, b_, :] = 1.0
    ALt[0, 32, :] = 16.0 * p
    ALt[1, 32, :] = 1.0
    ALt[2, 32, :] = 1.0
    c["ALt"] = ALt
    alr = np.zeros((16, 3, 6, 2, 4, 128), np.float64)
    for i in range(16):
        for hk in range(2):
            for g_ in range(4):
                sl = slopes[4 * hk + g_]
                for a_ in range(2):
                    alr[i, 0, a_, hk, g_, :] = sl
                    alr[i, 1, a_, hk, g_, :] = sl
                    alr[i, 2, a_, hk, g_, :] = sl * 64.0 * (64 * a_ - 8 * i - 2 * j - 1)
                for t in range(4):
                    alr[i, 0, 2 + t, hk, g_, :] = sl
                    alr[i, 1, 2 + t, hk, g_, :] = sl * 64.0 * (32 * t - 8 * i - 2 * j - 1)
                    alr[i, 2, 2 + t, hk, g_, :] = -0.5 * sl
    c["alr"] = alr.reshape(16, 3, 6, 2, 512).astype(f)
    alr2 = np.zeros((16, 2, 3, 2, 4, 128), np.float64)
    for i in range(16):
        for hk in range(2):
            for g_ in range(4):
                sl = slopes[4 * hk + g_]
                for a_ in range(2):
                    alr2[i, hk, 0, a_, g_, :] = sl
                    alr2[i, hk, 1, a_, g_, :] = sl
                    alr2[i, hk, 2, a_, g_, :] = sl * 64.0 * (64 * a_ - 8 * i - 2 * j - 1)
    c["alr2"] = alr2.reshape(16, 2, 3, 2, 512).astype(f)
    selA = np.zeros((16, 128, 128), f)
    selB = np.zeros((16, 128, 128), f)
    m = np.arange(128)
    for i in range(16):
        cur = 8 * i + 2 * j + (q >= 64).astype(np.int64)
        lag = cur[:, None] - m[None, :]
        forced = (m[None, :] == 0) | ((lag >= 0) & (lag < 2))
        selA[i] = ((lag >= 0) & (~forced)).astype(f)
        selB[i] = np.where(forced, 1e9, np.where(lag >= 0, 0.0, -1.0)).astype(f)
    c["selA"] = selA
    c["selB"] = selB
    return c


def _shared_constants():
    f = np.float32
    E64 = np.zeros((128, 32, 128), f)
    key = np.arange(128)
    for v in range(32):
        lr = 2 * v + key // 64
        E64[lr, v, key] = 1.0
        E64[64, v, :] = key
        E64[65, v, :] = 128.0 * v
        E64[66, v, :] = 1.0
    vca = np.zeros((128, 4, 2, 258), f)
    vca[:, :, :, 128] = 1.0
    m = np.arange(128)
    for t in range(4):
        n = 128 * t + np.arange(128) - 1
        ov = ((16 * n[:, None] + 31 >= 64 * m[None, :]) & (16 * n[:, None] <= 64 * m[None, :] + 63) & (n[:, None] >= 0)).astype(f)
        vca[:, t, 0, 129:257] = ov
        vca[:, t, 1, 129:257] = ov
    vca[0, 0, :, :] = 0.0
    return {"E64": E64, "vca": vca, "identc": np.eye(128, dtype=f)}


_CACHE = {}


def kernel(**inputs):
    debug = bool(int(os.environ.get("MK_DEBUG", "0")))
    f = np.float32
    x = np.asarray(inputs["x"], f)
    g = lambda k: np.ascontiguousarray(np.asarray(inputs[k], f)[0])
    shared = {
        "w_in": g("w_in"),
        "cw1k": g("cmp_w1_k"), "cw2k": g("cmp_w2_k"), "cpeTk": np.ascontiguousarray(g("cmp_pe_k").T),
        "cw1v": g("cmp_w1_v"), "cw2v": g("cmp_w2_v"), "cpeTv": np.ascontiguousarray(g("cmp_pe_v").T),
        "wpg": g("w_proj_gm"), "wpn": g("w_proj_nsa"), "wout": g("w_out"),
        "wff1": g("w_ff1"), "wff2": g("w_ff2"),
        "lngT": np.ascontiguousarray(g("gm_ln_g").reshape(8, 128).T), "lnbT": np.ascontiguousarray(g("gm_ln_b").reshape(8, 128).T),
        "wsT": np.ascontiguousarray(g("gm_w_s").transpose(2, 0, 1)),
        "gbs": np.ascontiguousarray(g("gm_b_s").reshape(1024)),
        "ln1g": g("ln1_g"), "ln1b": g("ln1_b"), "ln2g": g("ln2_g"), "ln2b": g("ln2_b"),
    }
    shared.update(_shared_constants())
    xTb = [np.ascontiguousarray(x[b].T) for b in range(2)]
    in_maps = []
    idxs = []
    for c in range(8):
        b, j = c // 4, c % 4
        idx = (512 * np.arange(16)[:, None] + 128 * j + np.arange(128)[None, :]).reshape(-1)
        idxs.append((b, idx))
        xo = np.ascontiguousarray(x[b][idx])
        m = dict(shared)
        m["xT"] = xTb[b]
        m["xo"] = xo
        m["xTo"] = np.ascontiguousarray(xo.T)
        m.update(_core_constants(j))
        in_maps.append(m)
    stop = os.environ.get("MK_STOP", "")
    key = ("nc", debug, stop)
    if key not in _CACHE:
        _CACHE[key] = build_program(debug, stop)
    nc = _CACHE[key]
    ncores = int(os.environ.get("MK_CORES", "8"))
    res = run_bass_kernel_spmd(nc, in_maps[:ncores], core_ids=list(range(ncores)))
    out = np.empty((2, 8192, 1024), f)
    out[:] = 0
    for c in range(ncores):
        b, idx = idxs[c]
        out[b, idx] = res.results[c]["out"]
    if debug:
        dbg = np.zeros((2, 8192, 1024), f)
        for c in range(ncores):
            b, idx = idxs[c]
            dbg[b, idx] = res.results[c]["dbg"]
        kernel.dbg = dbg
    return out
```

```python
import os
from contextlib import ExitStack

import numpy as np
import concourse.bass as bass
import concourse.mybir as mybir
from concourse.bass_utils import run_bass_kernel_spmd

F32 = mybir.dt.float32
BF16 = mybir.dt.bfloat16
AF = mybir.ActivationFunctionType
ALU = mybir.AluOpType

ENGINES = ("tensor", "vector", "scalar", "gpsimd", "sync")
NEG = -32768.0
ALPHA = 2.0 ** 0.25
QSCALE = 128.0 ** -0.5
LN_EPS = 1e-5


class Op:
    __slots__ = ("eng", "fn", "deps", "marked", "count", "is_dma", "dma_key", "dma_count", "idx")

    def __init__(self, eng, fn):
        self.eng = eng
        self.fn = fn
        self.deps = []
        self.marked = False
        self.count = 0
        self.is_dma = False
        self.dma_key = None
        self.dma_count = 0


class Prog:
    def __init__(self, nc):
        self.nc = nc
        self.ops = {e: [] for e in ENGINES}
        self.last_w = {}
        self.readers = {}
        self.dma_counts = {}
        self.dma_last = {}
        self.all_ops = []
        self.pending = {e: [] for e in ENGINES}

    def _add(self, eng, fn, reads, writes, dma_key=None):
        op = Op(eng, fn)
        if dma_key is not None:
            op.is_dma = True
            op.dma_key = dma_key
            self.dma_counts[dma_key] = self.dma_counts.get(dma_key, 0) + 16
            op.dma_count = self.dma_counts[dma_key]
            self.dma_last[dma_key] = op
        deps = list(self.pending[eng])
        self.pending[eng] = []
        for k in reads:
            w = self.last_w.get(k)
            if w is not None:
                deps.append(w)
            if isinstance(k, str) and k.startswith("bk"):
                for r in self.readers.get(k, ()):
                    if r.eng != eng:
                        deps.append(r)
        for k in writes:
            w = self.last_w.get(k)
            if w is not None:
                deps.append(w)
            deps.extend(self.readers.get(k, ()))
        op.idx = len(self.all_ops)
        best = {}
        for d in deps:
            if d is op:
                continue
            if (not d.is_dma) and (not op.is_dma) and d.eng == "tensor" and eng == "tensor":
                continue
            k = ("d", d.dma_key) if d.is_dma else ("e", d.eng)
            o = best.get(k)
            if o is None or d.idx > o.idx:
                best[k] = d
        op.deps = list(best.values())
        for k in reads:
            self.readers.setdefault(k, []).append(op)
        for k in writes:
            self.last_w[k] = op
            self.readers[k] = []
        self.all_ops.append(op)
        self.ops[eng].append(op)
        return op

    def op(self, eng, fn, reads=(), writes=()):
        return self._add(eng, fn, list(reads), list(writes))

    def dma(self, eng, fn, reads=(), writes=(), key=None):
        return self._add(eng, fn, list(reads), list(writes), dma_key=key)

    def barrier(self):
        snap = []
        for e in ENGINES:
            for op in reversed(self.ops[e]):
                if not op.is_dma:
                    snap.append(op)
                    break
        snap.extend(self.dma_last.values())
        for e in ENGINES:
            self.pending[e] = list(snap)

    def emit(self, final_waits=()):
        nc = self.nc
        for op in self.all_ops:
            for d in op.deps:
                if not d.is_dma:
                    d.marked = True
        for e in ENGINES:
            c = 0
            for op in self.ops[e]:
                if op.marked and not op.is_dma:
                    c += 1
                    op.count = c
        if os.environ.get("MK_VERBOSE"):
            print("sem counts", {e: max([o.count for o in self.ops[e]] + [0]) for e in ENGINES}, {e: len(self.ops[e]) for e in ENGINES}, "dma", max(self.dma_counts.values()))
        with ExitStack() as st:
            esem = {e: st.enter_context(nc.semaphore("p_" + e)) for e in ENGINES}
            dsem = {k: st.enter_context(nc.semaphore("d_" + str(k))) for k in self.dma_counts}
            block = st.enter_context(nc.Block())

            def run(engname):
                def body(eng):
                    waited = {}
                    for op in self.ops[engname]:
                        need = {}
                        for d in op.deps:
                            if d.is_dma:
                                k = ("d", d.dma_key)
                                v = d.dma_count
                            else:
                                k = ("e", d.eng)
                                v = d.count
                            if v > need.get(k, 0):
                                need[k] = v
                        for k, v in need.items():
                            if waited.get(k, 0) >= v:
                                continue
                            waited[k] = v
                            eng.wait_ge(dsem[k[1]] if k[0] == "d" else esem[k[1]], v)
                        ins = op.fn(eng)
                        if op.is_dma:
                            ins.then_inc(dsem[op.dma_key], 16)
                        elif op.marked:
                            ins.then_inc(esem[engname], 1)
                    if engname == "sync":
                        for k in final_waits:
                            eng.wait_ge(dsem[k], self.dma_counts[k])
                return body

            block.tensor(run("tensor"))
            block.vector(run("vector"))
            block.scalar(run("scalar"))
            block.gpsimd(run("gpsimd"))
            block.sync(run("sync"))


class Arena:
    def __init__(self, ap, nbytes):
        self.ap = ap
        self.nbytes = nbytes
        self.off = 0

    def alloc(self, free_shape, dtype):
        n = 1
        for s in free_shape:
            n *= s
        size = n * (4 if dtype == F32 else 2)
        self.off = (self.off + 63) // 64 * 64
        assert self.off + size <= self.nbytes, ("arena overflow", self.off, size, self.nbytes)
        sl = self.ap[:, self.off // 2:(self.off + size) // 2]
        self.off += size
        v = sl.bitcast(F32) if dtype == F32 else sl
        if len(free_shape) == 2:
            v = v.rearrange("p (a b) -> p a b", a=free_shape[0])
        elif len(free_shape) == 3:
            v = v.rearrange("p (a b c) -> p a b c", a=free_shape[0], b=free_shape[1])
        elif len(free_shape) == 4:
            v = v.rearrange("p (a b c d) -> p a b c d", a=free_shape[0], b=free_shape[1], c=free_shape[2])
        return v


class _Stop(Exception):
    pass


def build_program(debug=False, stop=""):
    nc = bass.Bass("TRN2", target_bir_lowering=False)

    def din(name, shape):
        return nc.dram_tensor(name, list(shape), F32, kind="ExternalInput").ap()

    xT = din("xT", [1024, 8192])
    xTo = din("xTo", [1024, 2048])
    xo = din("xo", [2048, 1024])
    w_in = din("w_in", [1024, 6680])
    cw1k = din("cw1k", [4096, 128]); cw2k = din("cw2k", [128, 128]); cpeTk = din("cpeTk", [128, 32])
    cw1v = din("cw1v", [4096, 128]); cw2v = din("cw2v", [128, 128]); cpeTv = din("cpeTv", [128, 32])
    wpg = din("wpg", [1024, 1024]); wpn = din("wpn", [1024, 1024]); wout = din("wout", [1024, 1024])
    wff1 = din("wff1", [1024, 4096]); wff2 = din("wff2", [4096, 1024])
    lngT_d = din("lngT", [128, 8]); lnbT_d = din("lnbT", [128, 8])
    wsT_d = din("wsT", [128, 8, 128]); gbs_d = din("gbs", [1024])
    ln1g_d = din("ln1g", [1024]); ln1b_d = din("ln1b", [1024]); ln2g_d = din("ln2g", [1024]); ln2b_d = din("ln2b", [1024])
    csel_d = din("csel", [128, 4, 128]); wsel_d = din("wsel", [128, 8, 128]); maskc_d = din("maskc", [32, 128])
    Lm_d = din("Lm", [32, 4, 128]); ALt_d = din("ALt", [3, 33, 128]); alr_d = din("alr", [16, 3, 6, 2, 512]); alr2_d = din("alr2", [16, 2, 3, 2, 512])
    selA_d = din("selA", [16, 128, 128]); selB_d = din("selB", [16, 128, 128])
    E64_d = din("E64", [128, 32, 128]); vca_d = din("vca", [128, 4, 2, 258]); ident_d = din("identc", [128, 128])
    out_d = nc.dram_tensor("out", [2048, 1024], F32, kind="ExternalOutput").ap()
    ynsa_d = nc.dram_tensor("ynsa_scr", [128, 16, 8, 128], BF16).ap()
    h_d = nc.dram_tensor("h_scr", [2048, 1024], F32).ap()
    dbg_d = None
    if debug:
        dbg_d = nc.dram_tensor("dbg", [2048, 1024], F32, kind="ExternalOutput").ap()

    xTv = xT.rearrange("(c p) t -> p c t", p=128)
    xTov = xTo.rearrange("(c p) t -> p c t", p=128)
    w_inv = w_in.rearrange("(c p) n -> p c n", p=128)

    ARENA_BYTES = 204 * 1024
    with ExitStack() as st:
        arena_t = st.enter_context(nc.sbuf_tensor("arena", [128, ARENA_BYTES // 2], BF16))
        banks = [st.enter_context(nc.psum_tensor("bk%d" % k, [128, 512], F32)) for k in range(8)]
        AR = Arena(arena_t[:], ARENA_BYTES)
        P = Prog(nc)
        bk = lambda k: "bk%d" % k
        cnt = {"ev": 0}

        def mm(out, lhsT, rhs, start, stop, reads, writes):
            P.op("tensor", lambda e: e.matmul(out, lhsT=lhsT, rhs=rhs, start=start, stop=stop), reads, writes)

        def tr(out, in_, reads, writes):
            P.op("tensor", lambda e: e.transpose(out=out, in_=in_, identity=ident[:]), list(reads) + ["ident"], writes)

        def act(out, in_, func, reads, writes, bias=None, scale=None):
            kw = {}
            if bias is not None:
                kw["bias"] = bias
            if scale is not None:
                kw["scale"] = scale
            P.op("scalar", lambda e: e.activation(out=out, in_=in_, func=func, **kw), reads, writes)

        def cp(eng, out, in_, reads, writes):
            if eng == "scalar":
                P.op("scalar", lambda e: e.copy(out=out, in_=in_), reads, writes)
            else:
                P.op(eng, lambda e: e.tensor_copy(out=out, in_=in_), reads, writes)

        def evac(out, in_, reads, writes):
            cnt["ev"] += 1
            cp("scalar" if cnt["ev"] % 2 else "vector", out, in_, reads, writes)

        def ts(eng, out, in0, s1, s2, op0, op1, reads, writes):
            if op1 is None:
                P.op(eng, lambda e: e.tensor_scalar(out=out, in0=in0, scalar1=s1, scalar2=None, op0=op0), reads, writes)
            else:
                P.op(eng, lambda e: e.tensor_scalar(out=out, in0=in0, scalar1=s1, scalar2=s2, op0=op0, op1=op1), reads, writes)

        def tt(eng, out, in0, in1, op, reads, writes):
            P.op(eng, lambda e: e.tensor_tensor(out=out, in0=in0, in1=in1, op=op), reads, writes)

        def stt(out, in0, scalar, in1, op0, op1, reads, writes):
            P.op("vector", lambda e: e.scalar_tensor_tensor(out=out, in0=in0, scalar=scalar, in1=in1, op0=op0, op1=op1), reads, writes)

        def vop(eng, name, reads, writes, **kw):
            P.op(eng, lambda e: getattr(e, name)(**kw), reads, writes)

        def castdma(out, in_, writes, key):
            last = out.shape[-1]
            if len(out.shape) >= 3 and last > 1024:
                for c0 in range(0, last, 1024):
                    castdma1(out[..., c0:c0 + 1024], in_[..., c0:c0 + 1024], writes, key)
            else:
                castdma1(out, in_, writes, key)

        def castdma1(out, in_, writes, key):
            P.dma("gpsimd", lambda e: e.dma_start(out=out, in_=in_, max_dma_last_dim=4096), (), writes, key=key)

        def ldma(out, in_, writes, key, reads=()):
            P.dma("sync", lambda e: e.dma_start(out=out, in_=in_), reads, writes, key=key)

        ident = AR.alloc([128], BF16)
        castdma(ident[:], ident_d, ["ident"], "ident")
        k_cmpT = AR.alloc([2, 512], BF16)
        vca = AR.alloc([4, 2, 258], BF16)
        castdma(vca[:], vca_d, ["vca"], "vca")
        persist_mark = AR.off

        try:
            Wc = AR.alloc([8, 512], BF16)
            w1k = AR.alloc([32, 128], BF16); w1v = AR.alloc([32, 128], BF16)
            w2k = AR.alloc([128], BF16); w2v = AR.alloc([128], BF16)
            peTk = AR.alloc([32], BF16); peTv = AR.alloc([32], BF16)
            biasK = AR.alloc([1], F32); biasV = AR.alloc([1], F32)
            kc_all = AR.alloc([2, 8208], BF16); vc_all = AR.alloc([2, 8208], BF16)
            xg = AR.alloc([8, 512], BF16)
            hT = AR.alloc([2, 128], BF16)

            castdma(Wc[:], w_inv[:, :, 3072:3584], ["Wc"], "Wc")
            castdma(w1k[:], cw1k.rearrange("(p d) f -> d p f", d=128), ["w1k"], "w1k")
            castdma(w1v[:], cw1v.rearrange("(p d) f -> d p f", d=128), ["w1v"], "w1v")
            castdma(w2k[:], cw2k, ["w2k"], "w2k")
            castdma(w2v[:], cw2v, ["w2v"], "w2v")
            castdma(peTk[:], cpeTk, ["peTk"], "peTk")
            castdma(peTv[:], cpeTv, ["peTv"], "peTv")
            P.op("vector", lambda e: e.memset(kc_all[:, :, 0:16], 0.0), (), ["kc_all"])
            P.op("vector", lambda e: e.memset(vc_all[:, :, 0:16], 0.0), (), ["vc_all"])
            for (w1, peT, bias_t, nm, b) in ((w1k, peTk, biasK, "k", 6), (w1v, peTv, biasV, "v", 7)):
                for p in range(32):
                    mm(banks[b][:, 0:1], w1[:, p, :], peT[:, p:p + 1], p == 0, p == 31, ["w1" + nm, "peT" + nm], [bk(b)])
                cp("vector", bias_t[:], banks[b][:, 0:1], [bk(b)], ["bias" + nm])

            xg2 = AR.alloc([8, 512], BF16)
            for i in range(16):
                xg_ = (xg, xg2)[i % 2]
                xk = "xg%d" % (i % 2)
                castdma(xg_[:], xTv[:, :, 512 * i:512 * i + 512], [xk], xk)
                for cg in range(4):
                    b = cg
                    for dc in range(8):
                        mm(banks[b][:, :], Wc[:, dc, 128 * cg:128 * cg + 128], xg_[:, dc, :], dc == 0, dc == 7, ["Wc", xk], [bk(b)])
                    dst = (kc_all if cg < 2 else vc_all)
                    evac(dst[:, cg % 2, 16 + 512 * i:16 + 512 * i + 512], banks[b][:, :], [bk(b)], ["kc_all" if cg < 2 else "vc_all"])
            for t in range(4):
                for isk in (True, False):
                    w1, w2, raw, bias_t, nm = (w1k, w2k, kc_all, biasK, "k") if isk else (w1v, w2v, vc_all, biasV, "v")
                    b = 4 if isk else 5
                    rawkey = "kc_all" if isk else "vc_all"
                    for p in range(32):
                        lo = p + 2048 * t
                        mm(banks[b][:, 0:256], w1[:, p, :], raw[:, :, lo:lo + 16 * 127 + 1:16], p == 0, p == 31, ["w1" + nm, rawkey], [bk(b)])
                    act(hT[:].rearrange("p a b -> p (a b)"), banks[b][:, 0:256], AF.Gelu_apprx_tanh, [bk(b), "bias" + nm], ["hT"], bias=bias_t[:, 0:1])
                    if isk:
                        mm(banks[6][:, 0:256], w2[:, :], hT[:].rearrange("p a b -> p (a b)"), True, True, ["w2k", "hT"], [bk(6)])
                        evac(k_cmpT[:, :, 128 * t:128 * t + 128], banks[6][:, 0:256].rearrange("p (a b) -> p a b", a=2), [bk(6)], ["k_cmpT"])
                    else:
                        for hk in range(2):
                            mm(banks[7][:, 128 * hk:128 * hk + 128], hT[:, hk, :], w2[:, :], True, True, ["w2v", "hT"], [bk(7)])
                        evac(vca[:, t, :, 0:128], banks[7][:, 0:256].rearrange("p (a b) -> p a b", a=2), [bk(7)], ["vca"])

            vop0 = vca[0:1, 0, :, 0:128]
            P.op("vector", lambda e: e.memset(vop0, 0.0), ["vca"], ["vca"])
            if stop == "A0":
                if debug:
                    dtmp = AR.alloc([3, 1024], F32)
                    cp("vector", dtmp[:, 0, :], k_cmpT[:].rearrange("p a b -> p (a b)"), ["k_cmpT"], ["dtmp"])
                    vflat = vca[:].rearrange("p a b c -> p (a b c)")
                    cp("vector", dtmp[:, 1, :], vflat[:, 0:1024], ["vca"], ["dtmp"])
                    cp("vector", dtmp[:, 2, :], vflat[:, 1024:2048], ["vca"], ["dtmp"])
                    for k3 in range(3):
                        ldma(dbg_d[128 * k3:128 * k3 + 128, :], dtmp[:, k3, :], [], "dbg", reads=["dtmp"])
                raise _Stop()
            P.barrier()
            AR.off = persist_mark
            Wkv = AR.alloc([8, 1024], BF16)
            Wq = AR.alloc([8, 1024], BF16)
            Wg = AR.alloc([8, 24], BF16)
            ksT = AR.alloc([2, 8192], BF16)
            vs_aug = AR.alloc([64, 2, 130], BF16)
            kwT = AR.alloc([2, 2, 512], BF16)
            vw_aug = AR.alloc([2, 4, 2, 130], BF16)
            xg = AR.alloc([8, 512], BF16)
            xq = AR.alloc([8, 128], BF16)
            qT = AR.alloc([8, 128], BF16)
            E64 = AR.alloc([32, 128], BF16)
            Rc = [AR.alloc([2, 512], BF16) for _ in range(2)]
            Rw = [AR.alloc([2, 512], BF16) for _ in range(2)]
            csel = AR.alloc([4, 128], BF16)
            wsel = AR.alloc([8, 128], BF16)
            maskc = AR.alloc([128], BF16)
            Lm = AR.alloc([4, 128], BF16)
            ALt = AR.alloc([33, 128], BF16)
            alr = [AR.alloc([6, 2, 512], BF16) for _ in range(2)]
            selA = [AR.alloc([128], F32) for _ in range(2)]
            selB = [AR.alloc([128], F32) for _ in range(2)]
            PT = [AR.alloc([4, 128], BF16) for _ in range(4)]
            y_acc = AR.alloc([8, 128], F32)
            ybf = AR.alloc([8, 128], BF16)
            ynT = AR.alloc([8, 128], BF16)
            gates = AR.alloc([24], F32)
            imp = AR.alloc([128], F32)
            sc = AR.alloc([128], F32)
            sc2 = AR.alloc([128], F32)
            m8a = AR.alloc([8], F32); m8b = AR.alloc([8], F32)
            selb = AR.alloc([128], BF16)
            selbT = AR.alloc([128], BF16)
            zz = AR.alloc([4], F32); rz = AR.alloc([4], F32); coef = AR.alloc([4], F32)
            osb = AR.alloc([4, 258], F32)

            castdma(Wkv[:, :, 0:256], w_inv[:, :, 3584:3840], ["Wkv"], "Wkv")
            castdma(Wkv[:, :, 256:512], w_inv[:, :, 4096:4352], ["Wkv"], "Wkv")
            castdma(Wkv[:, :, 512:768], w_inv[:, :, 3840:4096], ["Wkv"], "Wkv")
            castdma(Wkv[:, :, 768:1024], w_inv[:, :, 4352:4608], ["Wkv"], "Wkv")
            castdma(Wq[:], w_inv[:, :, 2048:3072], ["Wq"], "Wq")
            castdma(Wg[:], w_inv[:, :, 4608:4632], ["Wg"], "Wg")
            castdma(E64[:], E64_d, ["E64"], "E64")
            castdma(csel[:], csel_d, ["csel"], "csel")
            castdma(wsel[:], wsel_d, ["wsel"], "wsel")
            castdma(maskc[0:32, :], maskc_d, ["maskc"], "maskc")
            castdma(Lm[0:32], Lm_d, ["Lm"], "Lm")
            castdma(ALt[0:3], ALt_d, ["ALt"], "ALt")
            P.op("gpsimd", lambda e: e.memset(vs_aug[:], 1.0), (), ["vs_aug"])
            for k2 in range(2):
                vop("gpsimd", "memset", (), ["Rc%d" % k2], ap=Rc[k2][:], constant=0.0)
                vop("gpsimd", "memset", (), ["Rw%d" % k2], ap=Rw[k2][:], constant=0.0)
            P.op("gpsimd", lambda e: e.memset(vw_aug[:], 1.0), (), ["vw_aug"])

            if os.environ.get("MK_SUB", "") == "setup":
                raise _Stop()
            def bcast4(ap2d):
                return ap2d.unsqueeze(1).to_broadcast([ap2d.shape[0], 4, 128])

            def finalize_branch(hk, br, first):
                W = 257 if br == 0 else 129
                for g in range(4):
                    cp("scalar" if g % 2 == 0 else "vector", osb[:, g, 0:W], banks[2 + g][:, 0:W], [bk(2 + g)], ["osb%d" % g])
                ts("vector", zz[:], osb[:, :, 128], 1e-30, None, ALU.max, None, ["osb0", "osb1", "osb2", "osb3"], ["zz"])
                P.op("vector", lambda e: e.reciprocal(out=rz[:], in_=zz[:]), ["zz"], ["rz"])
                gv = gates[:].rearrange("p (h r) -> p h r", r=3)[:, 4 * hk:4 * hk + 4, br]
                tt("vector", coef[:], rz[:], gv, ALU.mult, ["rz", "gates"], ["coef"])
                for g in range(4):
                    h = 4 * hk + g
                    if first:
                        ts("vector", y_acc[:, h, :], osb[:, g, 0:128], coef[:, g:g + 1], None, ALU.mult, None,
                           ["osb%d" % g, "coef"], ["y_acc"])
                    else:
                        stt(y_acc[:, h, :], osb[:, g, 0:128], coef[:, g:g + 1], y_acc[:, h, :], ALU.mult, ALU.add,
                            ["osb%d" % g, "coef", "y_acc"], ["y_acc"])

            sbank = [0, 1, 6]
            scount = {"n": 0}

            def next_sbank():
                scount["n"] += 1
                return sbank[scount["n"] % 3]

            pcount = {"n": 0}

            def next_PT():
                pcount["n"] += 1
                k = pcount["n"] % 4
                return PT[k], "PT%d" % k

            NG = int(os.environ.get("MK_NG", "16"))
            for i in range(NG):
                par = i % 2
                if i == 0:
                    castdma(xg[:], xTv[:, :, 0:512], ["xg"], "xgA")
                    castdma(xq[:], xTov[:, :, 0:128], ["xq"], "xq")
                ldma(selA[par][:], selA_d[i], ["selA%d" % par], "selA%d" % par)
                ldma(selB[par][:], selB_d[i], ["selB%d" % par], "selB%d" % par)
                castdma(alr[par][0:3], alr_d[i], ["alr%d" % par], "alr%d" % par)
                for cg in range(4):
                    b = (0, 1, 6, 7)[cg]
                    for dc in range(8):
                        mm(banks[b][:, :], Wkv[:, dc, 128 * cg:128 * cg + 128], xg[:, dc, :], dc == 0, dc == 7, ["Wkv", "xg"], [bk(b)])
                    if cg < 2:
                        evac(ksT[:, cg, 512 * i:512 * i + 512], banks[b][:, :], [bk(b)], ["ksT"])
                    else:
                        evac(kwT[:, par, cg - 2, :], banks[b][:, :], [bk(b)], ["kwT%d" % par])
                if os.environ.get("MK_SUB", "") == "kproj":
                    raise _Stop()
                for sub in range(4):
                    b = (0, 1, 6, 7)[sub]
                    for dc in range(8):
                        mm(banks[b][:, :], xg[:, dc, 128 * sub:128 * sub + 128], Wkv[:, dc, 512:1024], dc == 0, dc == 7, ["Wkv", "xg"], [bk(b)])
                    mkx = os.environ.get("MK_X", "")
                    if mkx not in ("1", "3"):
                        evac(vs_aug[:, 4 * i + sub, :, 0:128], banks[b][:, 0:256].rearrange("p (a b) -> p a b", a=2), [bk(b)], ["vs_aug"])
                    if mkx not in ("2", "3"):
                        evac(vw_aug[:, par, sub, :, 0:128], banks[b][:, 256:512].rearrange("p (a b) -> p a b", a=2), [bk(b)], ["vw_aug%d" % par])
                if i + 1 < NG:
                    castdma(xg[:], xTv[:, :, 512 * (i + 1):512 * (i + 1) + 512], ["xg"], "xgA")
                if os.environ.get("MK_SUB", "") == "vproj":
                    raise _Stop()
                for hb in range(2):
                    b = 6 + hb
                    for g in range(4):
                        h = 4 * hb + g
                        for dc in range(8):
                            mm(banks[b][:, 128 * g:128 * g + 128], Wq[:, dc, 128 * h:128 * h + 128], xq[:, dc, :], dc == 0, dc == 7, ["Wq", "xq"], [bk(b)])
                    act(qT[:, 4 * hb:4 * hb + 4, :].rearrange("p a b -> p (a b)"), banks[b][:, :], AF.Copy, [bk(b)], ["qT"], scale=QSCALE)
                if os.environ.get("MK_SUB", "") == "qproj":
                    raise _Stop()
                for dc in range(8):
                    mm(banks[0][:, 0:24], xq[:, dc, :], Wg[:, dc, :], dc == 0, dc == 7, ["Wg", "xq"], [bk(0)])
                act(gates[:], banks[0][:, 0:24], AF.Sigmoid, [bk(0)], ["gates"])
                if i + 1 < NG:
                    castdma(xq[:], xTov[:, :, 128 * (i + 1):128 * (i + 1) + 128], ["xq"], "xq")

                if os.environ.get("MK_SUB", "") == "proj":
                    raise _Stop()
                for hk in range(2):
                    q4 = qT[:, 4 * hk:4 * hk + 4, :].rearrange("p a b -> p (a b)")
                    units = []
                    p2 = (2 * i + hk) % 2
                    castdma(Rc[p2][64:67, :, :], alr2_d[i, hk], ["Rc%d" % p2], "Rc%d" % p2)
                    castdma(Rw[p2][64:67, :, :], alr2_d[i, hk], ["Rw%d" % p2], "Rw%d" % p2)

                    nt = i // 4 + 1

                    def cmp_S(t, last, M):
                        sb_ = next_sbank()
                        mm(banks[sb_][0:M, :], k_cmpT[:, hk, 128 * t:128 * t + M], q4, True, False, ["k_cmpT", "qT"], [bk(sb_)])
                        mm(banks[sb_][0:M, :], ALt[0:3, 32, 0:M], alr[par][0:3, 2 + t, hk, :], False, not last, ["ALt", "alr%d" % par], [bk(sb_)])
                        if last:
                            mm(banks[sb_][0:M, :], Lm[0:32, i % 4, 0:M], bcast4(maskc[0:32, :]), False, True, ["Lm", "maskc"], [bk(sb_)])
                        pt, ptk = next_PT()
                        act(pt[0:M, :, :].rearrange("p a b -> p (a b)"), banks[sb_][0:M, :], AF.Exp, [bk(sb_)], [ptk])
                        return pt, ptk

                    def cmp_PV(t, last, M, pt, ptk):
                        for g in range(4):
                            mm(banks[2 + g][:, 0:257], pt[0:M, g, :], vca[0:M, t, hk, 0:257], t == 0, last, [ptk, "vca"], [bk(2 + g)])
                        if last:
                            finalize_branch(hk, 0, True)
                            for g in range(4):
                                if g == 0:
                                    ts("vector", imp[:], osb[:, 0, 129:257], rz[:, 0:1], None, ALU.mult, None, ["osb0", "rz"], ["imp"])
                                else:
                                    stt(imp[:], osb[:, g, 129:257], rz[:, g:g + 1], imp[:], ALU.mult, ALU.add, ["osb%d" % g, "rz", "imp"], ["imp"])
                            tt("vector", sc[:], imp[:], selA[par][:], ALU.mult, ["imp", "selA%d" % par], ["sc"])
                            tt("vector", sc[:], sc[:], selB[par][:], ALU.add, ["sc", "selB%d" % par], ["sc"])
                            P.op("vector", lambda e: e.max(out=m8a[:], in_=sc[:]), ["sc"], ["m8a"])
                            P.op("vector", lambda e: e.match_replace(out=sc2[:], in_to_replace=m8a[:], in_values=sc[:], imm_value=-1e30), ["sc", "m8a"], ["sc2"])
                            P.op("vector", lambda e: e.max(out=m8b[:], in_=sc2[:]), ["sc2"], ["m8b"])
                            ts("vector", selb[:], sc[:], m8b[:, 7:8], NEG, ALU.is_lt, ALU.mult, ["sc", "m8b"], ["selb"])
                            tb = banks[7][:].bitcast(BF16)
                            tr(tb[0:64, 0:128], selb[:, 0:64], ["selb"], [bk(7)])
                            tr(tb[0:64, 128:256], selb[:, 64:128], ["selb"], [bk(7)])
                            for a_ in range(2):
                                cp("vector", Rc[p2][0:64, a_, :].rearrange("p (g q) -> p g q", g=4),
                                   tb[0:64, 128 * a_:128 * a_ + 128].unsqueeze(1).to_broadcast([64, 4, 128]), [bk(7)], ["Rc%d" % p2])

                    for t in range(nt):
                        last = (t == nt - 1)
                        M = 32 * (i % 4 + 1) if last else 128
                        units.append((lambda t=t, last=last, M=M: cmp_S(t, last, M),
                                      lambda pt, ptk, t=t, last=last, M=M: cmp_PV(t, last, M, pt, ptk)))

                    r8s = list(range(8)) if i > 0 else list(range(4, 8))

                    def win_S(r8):
                        wpar = (i - 1) % 2 if r8 < 4 else par
                        sub = r8 % 4
                        kt = 4 * (i - 1) + r8
                        sb_ = next_sbank()
                        mm(banks[sb_][:, :], kwT[:, wpar, hk, 128 * sub:128 * sub + 128], q4, True, False, ["kwT%d" % wpar, "qT"], [bk(sb_)])
                        mm(banks[sb_][:, :], E64[:, kt % 32, :], Rw[p2][:, kt // 32, :], False, False, ["E64", "Rw%d" % p2], [bk(sb_)])
                        mm(banks[sb_][:, :], ident[:], bcast4(wsel[:, r8, :]), False, True, ["ident", "wsel"], [bk(sb_)])
                        pt, ptk = next_PT()
                        act(pt[:].rearrange("p a b -> p (a b)"), banks[sb_][:, :], AF.Exp, [bk(sb_)], [ptk])
                        return pt, ptk

                    def win_PV(r8, first, last, pt, ptk):
                        wpar = (i - 1) % 2 if r8 < 4 else par
                        sub = r8 % 4
                        for g in range(4):
                            mm(banks[2 + g][:, 0:129], pt[:, g, :], vw_aug[:, wpar, sub, hk, 0:129], first, last,
                               [ptk, "vw_aug%d" % wpar], [bk(2 + g)])
                        if last:
                            finalize_branch(hk, 2, False)

                    for idx, r8 in enumerate(r8s):
                        units.append((lambda r8=r8: win_S(r8),
                                      lambda pt, ptk, r8=r8, f=(idx == 0), l=(idx == len(r8s) - 1): win_PV(r8, f, l, pt, ptk)))

                    nk = 4 * i + 4

                    def sel_S(kt):
                        diag = kt >= 4 * i
                        sb_ = next_sbank()
                        mm(banks[sb_][:, :], ksT[:, hk, 128 * kt:128 * kt + 128], q4, True, False, ["ksT", "qT"], [bk(sb_)])
                        mm(banks[sb_][:, :], E64[:, kt % 32, :], Rc[p2][:, kt // 32, :], False, not diag, ["E64", "Rc%d" % p2], [bk(sb_)])
                        if diag:
                            mm(banks[sb_][:, :], ident[:], bcast4(csel[:, kt - 4 * i, :]), False, True, ["ident", "csel"], [bk(sb_)])
                        pt, ptk = next_PT()
                        act(pt[:].rearrange("p a b -> p (a b)"), banks[sb_][:, :], AF.Exp, [bk(sb_)], [ptk])
                        return pt, ptk

                    def sel_PV(kt, pt, ptk):
                        for g in range(4):
                            mm(banks[2 + g][:, 0:129], pt[:, g, :], vs_aug[:, kt, hk, 0:129], kt == 0, kt == nk - 1, [ptk, "vs_aug"], [bk(2 + g)])
                        if kt == nk - 1:
                            finalize_branch(hk, 1, False)

                    for kt in range(nk):
                        units.append((lambda kt=kt: sel_S(kt), lambda pt, ptk, kt=kt: sel_PV(kt, pt, ptk)))

                    LA = 2
                    pend = []
                    for (fS, fPV) in units:
                        pend.append((fPV, fS()))
                        if len(pend) > LA:
                            f_, r_ = pend.pop(0)
                            f_(*r_)
                    for f_, r_ in pend:
                        f_(*r_)
                if os.environ.get("MK_SUB", "") == "win":
                    raise _Stop()
                if debug:
                    ldma(dbg_d[128 * i:128 * i + 128, :], y_acc[:].rearrange("p a b -> p (a b)"), [], "dbg", reads=["y_acc"])
                cp("scalar", ybf[:], y_acc[:], ["y_acc"], ["ybf"])
                tb = banks[7][:].bitcast(BF16)
                for c in range(8):
                    tr(tb[:, 128 * c:128 * c + 128], ybf[:, c, :], ["ybf"], [bk(7)])
                cp("vector", ynT[:].rearrange("p a b -> p (a b)"), tb[:, :], [bk(7)], ["ynT"])
                ldma(ynsa_d[:, i, :, :], ynT[:], ["ynsa_d"], "ynsa_w", reads=["ynT"])

            if stop == "A":
                raise _Stop()
            P.barrier()
            AR.off = persist_mark
            ygT_all = AR.alloc([8, 2048], BF16)
            b1a_mark = AR.off
            Wu = AR.alloc([8, 1024], BF16)
            Wv = AR.alloc([8, 1024], BF16)
            xt = AR.alloc([8, 512], BF16)
            uT = AR.alloc([8, 512], BF16)
            vg = [AR.alloc([1024], F32) for _ in range(2)]
            vn = [AR.alloc([1024], BF16) for _ in range(2)]
            WcT = AR.alloc([8, 128], BF16)
            wsf = AR.alloc([8, 128], F32)
            Badd = AR.alloc([8, 128], F32)
            bsrep = AR.alloc([8, 128], F32)
            lngT = AR.alloc([8], F32); lnbT = AR.alloc([8], F32)
            ones = AR.alloc([128], BF16)
            stat = AR.alloc([2, 6], F32); mv = AR.alloc([2], F32); rstd = AR.alloc([1], F32)
            t1 = [AR.alloc([128], F32) for _ in range(2)]

            castdma(Wu[:], w_inv[:, :, 0:1024], ["Wu"], "Wu")
            castdma(Wv[:], w_inv[:, :, 1024:2048], ["Wv"], "Wv")
            ldma(wsf[:], wsT_d, ["wsf"], "wsf")
            ldma(bsrep[:].rearrange("p a b -> p (a b)"), gbs_d.partition_broadcast(128), ["bsrep"], "bsrep")
            ldma(lngT[:], lngT_d, ["lngT"], "lngT")
            ldma(lnbT[:], lnbT_d, ["lnbT"], "lnbT")
            P.op("vector", lambda e: e.memset(ones[:], 1.0), (), ["ones"])
            P.op("gpsimd", lambda e: e.affine_select(out=wsf[:], in_=wsf[:], pattern=[[0, 8], [1, 128]], compare_op=ALU.is_ge,
                                                     fill=0.0, base=0, channel_multiplier=-1), ["wsf"], ["wsf"])
            cp("vector", WcT[:], wsf[:], ["wsf"], ["WcT"])
            for g in range(8):
                b = g % 2
                mm(banks[b][:, 0:128], ones[:], WcT[:, g, :], True, True, ["ones", "WcT"], [bk(b)])
                stt(Badd[:, g, :], banks[b][:, 0:128], lnbT[:, g:g + 1], bsrep[:, g, :], ALU.mult, ALU.add,
                    [bk(b), "lnbT", "bsrep"], ["Badd"])

            def layer_norm_stats(src, srckey, stat, mv, rstd):
                vop("vector", "bn_stats", [srckey], ["stat"], out=stat[:, 0, :], in_=src[:, 0:512])
                vop("vector", "bn_stats", [srckey, "stat"], ["stat"], out=stat[:, 1, :], in_=src[:, 512:1024])
                vop("vector", "bn_aggr", ["stat"], ["mv"], out=mv[:], in_=stat[:].rearrange("p a b -> p (a b)"))
                act(rstd[:], mv[:, 1:2], AF.Sqrt, ["mv"], ["rstd"], bias=LN_EPS, scale=1.0)
                vop("vector", "reciprocal", ["rstd"], ["rstd"], out=rstd[:], in_=rstd[:])

            xtB = AR.alloc([8, 512], BF16)
            xtA = xt
            uTB = AR.alloc([8, 512], BF16)
            uTA = uT
            statB = AR.alloc([2, 6], F32); mvB = AR.alloc([2], F32); rstdB = AR.alloc([1], F32)

            def b1a_front(T, sub):
                xt_ = (xtA, xtB)[T % 2]
                xtk = "xt%d" % (T % 2)
                uT_ = (uTA, uTB)[T % 2]
                uk = "uT%d" % (T % 2)
                if sub == 0:
                    castdma(xt_[:], xTov[:, :, 512 * T:512 * T + 512], [xtk], xtk)
                    for cc in range(8):
                        b = cc % 4
                        for dc in range(8):
                            mm(banks[b][:, :], Wu[:, dc, 128 * cc:128 * cc + 128], xt_[:, dc, :], dc == 0, dc == 7, ["Wu", xtk], [bk(b)])
                        act(uT_[:, cc, :], banks[b][:, :], AF.Gelu_apprx_tanh, [bk(b)], [uk])
                v_ = vg[sub % 2]; vk = "vg%d" % (sub % 2)
                n_ = vn[sub % 2]; nk_ = "vn%d" % (sub % 2)
                st_, mv_, rs_ = ((stat, mv, rstd), (statB, mvB, rstdB))[sub % 2]
                sfx = "_%d" % (sub % 2)
                for half in range(2):
                    b = 4 + half
                    for dc in range(8):
                        mm(banks[b][:, :], xt_[:, dc, 128 * sub:128 * sub + 128], Wv[:, dc, 512 * half:512 * half + 512], dc == 0, dc == 7,
                           ["Wv", xtk], [bk(b)])
                    act(v_[:, 512 * half:512 * half + 512], banks[b][:, :], AF.Gelu_apprx_tanh, [bk(b)], [vk])
                vop("vector", "bn_stats", [vk], ["stat" + sfx], out=st_[:, 0, :], in_=v_[:, 0:512])
                vop("vector", "bn_stats", [vk, "stat" + sfx], ["stat" + sfx], out=st_[:, 1, :], in_=v_[:, 512:1024])
                vop("vector", "bn_aggr", ["stat" + sfx], ["mv" + sfx], out=mv_[:], in_=st_[:].rearrange("p a b -> p (a b)"))
                act(rs_[:], mv_[:, 1:2], AF.Sqrt, ["mv" + sfx], ["rstd" + sfx], bias=LN_EPS, scale=1.0)
                vop("vector", "reciprocal", ["rstd" + sfx], ["rstd" + sfx], out=rs_[:], in_=rs_[:])
                ts("vector", n_[:], v_[:], mv_[:, 0:1], rs_[:, 0:1], ALU.subtract, ALU.mult, [vk, "mv" + sfx, "rstd" + sfx], [nk_])

            def b1a_back(T, sub):
                uT_ = (uTA, uTB)[T % 2]
                uk = "uT%d" % (T % 2)
                n_ = vn[sub % 2]; nk_ = "vn%d" % (sub % 2)
                for g in range(8):
                    b = 6 + (g // 4) % 2
                    mm(banks[b][:, 128 * (g % 4):128 * (g % 4) + 128], n_[:, 128 * g:128 * g + 128], WcT[:, g, :], True, True,
                       [nk_, "WcT"], [bk(b)])
                    if g % 4 == 3:
                        for g2 in range(g - 3, g + 1):
                            tk = t1[g2 % 2]; tkk = "t1%d" % (g2 % 2)
                            stt(tk[:], banks[b][:, 128 * (g2 % 4):128 * (g2 % 4) + 128], lngT[:, g2:g2 + 1], Badd[:, g2, :], ALU.mult, ALU.add,
                                [bk(b), "lngT", "Badd"], [tkk])
                            tt("gpsimd", ygT_all[:, g2, 512 * T + 128 * sub:512 * T + 128 * sub + 128], tk[:], uT_[:, g2, 128 * sub:128 * sub + 128],
                               ALU.mult, [tkk, uk], ["ygT_all"])

            seq = [(T, sub) for T in range(4) for sub in range(4)]
            for k_, (T, sub) in enumerate(seq):
                b1a_front(T, sub)
                if k_ >= 1:
                    b1a_back(*seq[k_ - 1])
            b1a_back(*seq[-1])

            if stop == "B1a":
                raise _Stop()
            P.barrier()
            AR.off = b1a_mark
            Wm = AR.alloc([8, 2048], BF16)
            Wpg = AR.alloc([8, 1024], BF16); Wpn = AR.alloc([8, 1024], BF16); Wo = AR.alloc([8, 1024], BF16)
            xt = AR.alloc([8, 512], BF16)
            ynt = AR.alloc([8, 4, 128], BF16)
            sg = [AR.alloc([512], F32) for _ in range(2)]
            m01 = [AR.alloc([512], F32) for _ in range(2)]
            mrgT = AR.alloc([8, 512], BF16)
            xot = AR.alloc([1024], F32)
            r1 = AR.alloc([1024], F32)
            lg = AR.alloc([1024], F32); lb = AR.alloc([1024], F32)
            stat = AR.alloc([2, 6], F32); mv = AR.alloc([2], F32); rstd = AR.alloc([1], F32)

            castdma(Wpg[:], wpg.rearrange("(c p) n -> p c n", p=128), ["Wpg"], "Wpg")
            castdma(Wpn[:], wpn.rearrange("(c p) n -> p c n", p=128), ["Wpn"], "Wpn")
            castdma(Wm[:], w_inv[:, :, 4632:6680], ["Wm"], "Wm")
            castdma(Wo[:], wout.rearrange("(c p) n -> p c n", p=128), ["Wo"], "Wo")
            ldma(lg[:], ln1g_d.partition_broadcast(128), ["lg"], "lg")
            ldma(lb[:], ln1b_d.partition_broadcast(128), ["lb"], "lb")

            def ln_tail(src, srckey, dst_dram, tag, stat, mv, rstd, lg, lb):
                layer_norm_stats(src, srckey, stat, mv, rstd)
                ts("vector", src[:], src[:], mv[:, 0:1], rstd[:, 0:1], ALU.subtract, ALU.mult, [srckey, "mv", "rstd"], [srckey])
                tt("gpsimd", src[:], src[:], lg[:], ALU.mult, [srckey, "lg"], [srckey])
                tt("gpsimd", src[:], src[:], lb[:], ALU.add, [srckey, "lb"], [srckey])
                ldma(dst_dram, src[:], [tag + "_dram"], tag, reads=[srckey])

            xtB = AR.alloc([8, 512], BF16)
            xtA = xt
            yntB = AR.alloc([8, 4, 128], BF16)
            yntA = ynt
            r1B = AR.alloc([1024], F32)
            r1A = r1
            for T in range(4):
                xt = (xtA, xtB)[T % 2]
                xtk = "xtb%d" % (T % 2)
                ynt = (yntA, yntB)[T % 2]
                yntk = "ynt%d" % (T % 2)
                castdma(xt[:], xTov[:, :, 512 * T:512 * T + 512], [xtk], xtk)
                for il in range(4):
                    ldma(ynt[:, :, il, :], ynsa_d[:, 4 * T + il, :, :], [yntk], yntk, reads=["ynsa_d"])
                for Dc in range(8):
                    s4 = 4 * (Dc % 2)
                    for dc in range(8):
                        mm(banks[s4 + 0][:, :], Wpg[:, dc, 128 * Dc:128 * Dc + 128], ygT_all[:, dc, 512 * T:512 * T + 512], dc == 0, dc == 7,
                           ["Wpg", "ygT_all"], [bk(s4 + 0)])
                    for dc in range(8):
                        mm(banks[s4 + 1][:, :], Wpn[:, dc, 128 * Dc:128 * Dc + 128], ynt[:, dc, :, :].rearrange("p a b -> p (a b)"), dc == 0, dc == 7,
                           ["Wpn", yntk], [bk(s4 + 1)])
                    for br in range(2):
                        for dc in range(8):
                            mm(banks[s4 + 2 + br][:, :], Wm[:, dc, 1024 * br + 128 * Dc:1024 * br + 128 * Dc + 128], xt[:, dc, :], dc == 0, dc == 7,
                               ["Wm", xtk], [bk(s4 + 2 + br)])
                        act(sg[br][:], banks[s4 + 2 + br][:, :], AF.Sigmoid, [bk(s4 + 2 + br)], ["sg%d" % br])
                    tt("vector", m01[0][:], sg[0][:], banks[s4 + 0][:, :], ALU.mult, ["sg0", bk(s4 + 0)], ["m0"])
                    tt("vector", m01[1][:], sg[1][:], banks[s4 + 1][:, :], ALU.mult, ["sg1", bk(s4 + 1)], ["m1"])
                    tt("gpsimd", mrgT[:, Dc, :], m01[0][:], m01[1][:], ALU.add, ["m0", "m1"], ["mrgT"])
                for sub in range(4):
                    o0 = 512 * T + 128 * sub
                    r1 = (r1A, r1B)[sub % 2]
                    r1k = "r1%d" % (sub % 2)
                    ldma(xot[:], xo[o0:o0 + 128, :], ["xot"], "xot")
                    for half in range(2):
                        b = half
                        for Dc in range(8):
                            mm(banks[b][:, :], mrgT[:, Dc, 128 * sub:128 * sub + 128], Wo[:, Dc, 512 * half:512 * half + 512], Dc == 0, Dc == 7,
                               ["mrgT", "Wo"], [bk(b)])
                        stt(r1[:, 512 * half:512 * half + 512], xot[:, 512 * half:512 * half + 512], ALPHA, banks[b][:, :], ALU.mult, ALU.add,
                            ["xot", bk(b)], [r1k])
                    ln_tail(r1, r1k, h_d[o0:o0 + 128, :], "h_w", stat, mv, rstd, lg, lb)

            if stop == "B1b":
                raise _Stop()
            P.barrier()
            AR.off = persist_mark
            W1 = AR.alloc([8, 4096], BF16)
            W2 = AR.alloc([32, 1024], BF16)
            hin = [AR.alloc([1024], F32) for _ in range(2)]
            hbf = AR.alloc([1024], BF16)
            hT2 = AR.alloc([8, 256], BF16)
            aT = AR.alloc([32, 256], BF16)
            r2 = AR.alloc([1024], F32)
            r2B = AR.alloc([1024], F32)
            r2A = r2
            rta = AR.alloc([256], F32); rtv = AR.alloc([256], F32)
            lg = AR.alloc([1024], F32); lb = AR.alloc([1024], F32)
            stat = AR.alloc([2, 6], F32); mv = AR.alloc([2], F32); rstd = AR.alloc([1], F32)
            w1v_ = wff1.rearrange("(c p) n -> p c n", p=128)
            castdma(W1[:, :, 0:2048], w1v_[:, :, 0:2048], ["W1"], "W1")
            castdma(W1[:, :, 2048:4096], w1v_[:, :, 2048:4096], ["W1"], "W1")
            w2v_ = wff2.rearrange("(c p) n -> p c n", p=128)
            for q in range(4):
                castdma(W2[:, 8 * q:8 * q + 8, :], w2v_[:, 8 * q:8 * q + 8, :], ["W2"], "W2")
            ldma(lg[:], ln2g_d.partition_broadcast(128), ["lg"], "lg2")
            ldma(lb[:], ln2b_d.partition_broadcast(128), ["lb"], "lb2")

            for T8 in range(8):
                for s2 in range(2):
                    o0 = 256 * T8 + 128 * s2
                    ldma(hin[s2][:], h_d[o0:o0 + 128, :], ["hin%d" % s2], "hin%d" % s2, reads=["h_w_dram"])
                    cp("scalar", hbf[:], hin[s2][:], ["hin%d" % s2], ["hbf"])
                    tb = banks[6 + s2][:].bitcast(BF16)
                    for c in range(8):
                        tr(tb[:, 128 * c:128 * c + 128], hbf[:, 128 * c:128 * c + 128], ["hbf"], [bk(6 + s2)])
                    cp("vector", hT2[:, :, 128 * s2:128 * s2 + 128], tb[:, :].rearrange("p (a b) -> p a b", a=8), [bk(6 + s2)], ["hT2"])
                for fc in range(32):
                    b = fc % 4
                    for dc in range(8):
                        mm(banks[b][:, 0:256], W1[:, dc, 128 * fc:128 * fc + 128], hT2[:, dc, :], dc == 0, dc == 7, ["W1", "hT2"], [bk(b)])
                    if fc % 2 == 0:
                        act(rta[:], banks[b][:, 0:256], AF.Relu, [bk(b)], ["rta"])
                        tt("gpsimd", aT[:, fc, :], rta[:], rta[:], ALU.mult, ["rta"], ["aT"])
                    else:
                        ts("vector", rtv[:], banks[b][:, 0:256], 0.0, None, ALU.max, None, [bk(b)], ["rtv"])
                        tt("vector", aT[:, fc, :], rtv[:], rtv[:], ALU.mult, ["rtv"], ["aT"])
                for s2 in range(2):
                    o0 = 256 * T8 + 128 * s2
                    r2 = (r2A, r2B)[s2]
                    r2k = "r2%d" % s2
                    for half in range(2):
                        b = 4 + half
                        for fc in range(32):
                            mm(banks[b][:, :], aT[:, fc, 128 * s2:128 * s2 + 128], W2[:, fc, 512 * half:512 * half + 512], fc == 0, fc == 31,
                               ["aT", "W2"], [bk(b)])
                        stt(r2[:, 512 * half:512 * half + 512], hin[s2][:, 512 * half:512 * half + 512], ALPHA, banks[b][:, :], ALU.mult, ALU.add,
                            ["hin%d" % s2, bk(b)], [r2k])
                    ln_tail(r2, r2k, out_d[o0:o0 + 128, :], "out_w", stat, mv, rstd, lg, lb)


        except _Stop:
            pass
        fin = list(P.dma_counts.keys())
        P.emit(final_waits=fin)
    return nc


def _core_constants(j):
    f = np.float32
    q = np.arange(128)
    key = np.arange(128)
    slopes = (2.0 ** (-(np.arange(8) + 1.0))).astype(np.float64)
    c = {}
    kp = 128 * np.arange(4)[None, :, None] + key[:, None, None]
    qp = 128 * j + q[None, None, :]
    c["csel"] = np.where(kp <= qp, 0.0, NEG).astype(f)
    kp8 = 128 * (np.arange(8)[None, :, None] - 4) + key[:, None, None]
    dist = qp - kp8
    c["wsel"] = np.where((dist >= 0) & (dist < 512), 0.0, NEG).astype(f)
    l = np.arange(32)
    c["maskc"] = np.where(16 * l[:, None] + 15 <= 128 * j + q[None, :], 0.0, NEG).astype(f)
    Lm = np.zeros((32, 4, 128), f)
    for r in range(4):
        Lm[l, r, 32 * r + l] = 1.0
    c["Lm"] = Lm
    p = np.arange(128)
    ALt = np.zeros((3, 33, 128), f)
    for b_ in range(32):
        ALt[0, b_, :] = p
        ALt[1, b_, :] = 128.0 * b_
        ALt[2, b_, :] = 1.0
    ALt[0, 32, :] = 16.0 * p
    ALt[1, 32, :] = 1.0
    ALt[2, 32, :] = 1.0
    c["ALt"] = ALt
    alr = np.zeros((16, 3, 6, 2, 4, 128), np.float64)
    for i in range(16):
        for hk in range(2):
            for g_ in range(4):
                sl = slopes[4 * hk + g_]
                for a_ in range(2):
                    alr[i, 0, a_, hk, g_, :] = sl
                    alr[i, 1, a_, hk, g_, :] = sl
                    alr[i, 2, a_, hk, g_, :] = sl * 64.0 * (64 * a_ - 8 * i - 2 * j - 1)
                for t in range(4):
                    alr[i, 0, 2 + t, hk, g_, :] = sl
                    alr[i, 1, 2 + t, hk, g_, :] = sl * 64.0 * (32 * t - 8 * i - 2 * j - 1)
                    alr[i, 2, 2 + t, hk, g_, :] = -0.5 * sl
    c["alr"] = alr.reshape(16, 3, 6, 2, 512).astype(f)
    alr2 = np.zeros((16, 2, 3, 2, 4, 128), np.float64)
    for i in range(16):
        for hk in range(2):
            for g_ in range(4):
                sl = slopes[4 * hk + g_]
                for a_ in range(2):
                    alr2[i, hk, 0, a_, g_, :] = sl
                    alr2[i, hk, 1, a_, g_, :] = sl
                    alr2[i, hk, 2, a_, g_, :] = sl * 64.0 * (64 * a_ - 8 * i - 2 * j - 1)
    c["alr2"] = alr2.reshape(16, 2, 3, 2, 512).astype(f)
    selA = np.zeros((16, 128, 128), f)
    selB = np.zeros((16, 128, 128), f)
    m = np.arange(128)
    for i in range(16):
        cur = 8 * i + 2 * j + (q >= 64).astype(np.int64)
        lag = cur[:, None] - m[None, :]
        forced = (m[None, :] == 0) | ((lag >= 0) & (lag < 2))
        selA[i] = ((lag >= 0) & (~forced)).astype(f)
        selB[i] = np.where(forced, 1e9, np.where(lag >= 0, 0.0, -1.0)).astype(f)
    c["selA"] = selA
    c["selB"] = selB
    return c


def _shared_constants():
    f = np.float32
    E64 = np.zeros((128, 32, 128), f)
    key = np.arange(128)
    for v in range(32):
        lr = 2 * v + key // 64
        E64[lr, v, key] = 1.0
        E64[64, v, :] = key
        E64[65, v, :] = 128.0 * v
        E64[66, v, :] = 1.0
    vca = np.zeros((128, 4, 2, 258), f)
    vca[:, :, :, 128] = 1.0
    m = np.arange(128)
    for t in range(4):
        n = 128 * t + np.arange(128) - 1
        ov = ((16 * n[:, None] + 31 >= 64 * m[None, :]) & (16 * n[:, None] <= 64 * m[None, :] + 63) & (n[:, None] >= 0)).astype(f)
        vca[:, t, 0, 129:257] = ov
        vca[:, t, 1, 129:257] = ov
    vca[0, 0, :, :] = 0.0
    return {"E64": E64, "vca": vca, "identc": np.eye(128, dtype=f)}


_CACHE = {}


def kernel(**inputs):
    debug = bool(int(os.environ.get("MK_DEBUG", "0")))
    f = np.float32
    x = np.asarray(inputs["x"], f)
    g = lambda k: np.ascontiguousarray(np.asarray(inputs[k], f)[0])
    shared = {
        "w_in": g("w_in"),
        "cw1k": g("cmp_w1_k"), "cw2k": g("cmp_w2_k"), "cpeTk": np.ascontiguousarray(g("cmp_pe_k").T),
        "cw1v": g("cmp_w1_v"), "cw2v": g("cmp_w2_v"), "cpeTv": np.ascontiguousarray(g("cmp_pe_v").T),
        "wpg": g("w_proj_gm"), "wpn": g("w_proj_nsa"), "wout": g("w_out"),
        "wff1": g("w_ff1"), "wff2": g("w_ff2"),
        "lngT": np.ascontiguousarray(g("gm_ln_g").reshape(8, 128).T), "lnbT": np.ascontiguousarray(g("gm_ln_b").reshape(8, 128).T),
        "wsT": np.ascontiguousarray(g("gm_w_s").transpose(2, 0, 1)),
        "gbs": np.ascontiguousarray(g("gm_b_s").reshape(1024)),
        "ln1g": g("ln1_g"), "ln1b": g("ln1_b"), "ln2g": g("ln2_g"), "ln2b": g("ln2_b"),
    }
    shared.update(_shared_constants())
    xTb = [np.ascontiguousarray(x[b].T) for b in range(2)]
    in_maps = []
    idxs = []
    for c in range(8):
        b, j = c // 4, c % 4
        idx = (512 * np.arange(16)[:, None] + 128 * j + np.arange(128)[None, :]).reshape(-1)
        idxs.append((b, idx))
        xo = np.ascontiguousarray(x[b][idx])
        m = dict(shared)
        m["xT"] = xTb[b]
        m["xo"] = xo
        m["xTo"] = np.ascontiguousarray(xo.T)
        m.update(_core_constants(j))
        in_maps.append(m)
    stop = os.environ.get("MK_STOP", "")
    key = ("nc", debug, stop)
    if key not in _CACHE:
        _CACHE[key] = build_program(debug, stop)
    nc = _CACHE[key]
    ncores = int(os.environ.get("MK_CORES", "8"))
    res = run_bass_kernel_spmd(nc, in_maps[:ncores], core_ids=list(range(ncores)))
    out = np.empty((2, 8192, 1024), f)
    out[:] = 0
    for c in range(ncores):
        b, idx = idxs[c]
        out[b, idx] = res.results[c]["out"]
    if debug:
        dbg = np.zeros((2, 8192, 1024), f)
        for c in range(ncores):
            b, idx = idxs[c]
            dbg[b, idx] = res.results[c]["dbg"]
        kernel.dbg = dbg
    return out
```

```python
import os
from contextlib import ExitStack

import numpy as np
import concourse.bass as bass
import concourse.mybir as mybir
from concourse.bass_utils import run_bass_kernel_spmd

F32 = mybir.dt.float32
BF16 = mybir.dt.bfloat16
AF = mybir.ActivationFunctionType
ALU = mybir.AluOpType

ENGINES = ("tensor", "vector", "scalar", "gpsimd", "sync")
NEG = -32768.0
ALPHA = 2.0 ** 0.25
QSCALE = 128.0 ** -0.5
LN_EPS = 1e-5


class Op:
    __slots__ = ("eng", "fn", "deps", "marked", "count", "is_dma", "dma_key", "dma_count", "idx")

    def __init__(self, eng, fn):
        self.eng = eng
        self.fn = fn
        self.deps = []
        self.marked = False
        self.count = 0
        self.is_dma = False
        self.dma_key = None
        self.dma_count = 0


class Prog:
    def __init__(self, nc):
        self.nc = nc
        self.ops = {e: [] for e in ENGINES}
        self.last_w = {}
        self.readers = {}
        self.dma_counts = {}
        self.dma_last = {}
        self.all_ops = []
        self.pending = {e: [] for e in ENGINES}

    def _add(self, eng, fn, reads, writes, dma_key=None):
        op = Op(eng, fn)
        if dma_key is not None:
            op.is_dma = True
            op.dma_key = dma_key
            self.dma_counts[dma_key] = self.dma_counts.get(dma_key, 0) + 16
            op.dma_count = self.dma_counts[dma_key]
            self.dma_last[dma_key] = op
        deps = list(self.pending[eng])
        self.pending[eng] = []
        for k in reads:
            w = self.last_w.get(k)
            if w is not None:
                deps.append(w)
            if isinstance(k, str) and k.startswith("bk"):
                for r in self.readers.get(k, ()):
                    if r.eng != eng:
                        deps.append(r)
        for k in writes:
            w = self.last_w.get(k)
            if w is not None:
                deps.append(w)
            deps.extend(self.readers.get(k, ()))
        op.idx = len(self.all_ops)
        best = {}
        for d in deps:
            if d is op:
                continue
            if (not d.is_dma) and (not op.is_dma) and d.eng == "tensor" and eng == "tensor":
                continue
            k = ("d", d.dma_key) if d.is_dma else ("e", d.eng)
            o = best.get(k)
            if o is None or d.idx > o.idx:
                best[k] = d
        op.deps = list(best.values())
        for k in reads:
            self.readers.setdefault(k, []).append(op)
        for k in writes:
            self.last_w[k] = op
            self.readers[k] = []
        self.all_ops.append(op)
        self.ops[eng].append(op)
        return op

    def op(self, eng, fn, reads=(), writes=()):
        return self._add(eng, fn, list(reads), list(writes))

    def dma(self, eng, fn, reads=(), writes=(), key=None):
        return self._add(eng, fn, list(reads), list(writes), dma_key=key)

    def barrier(self):
        snap = []
        for e in ENGINES:
            for op in reversed(self.ops[e]):
                if not op.is_dma:
                    snap.append(op)
                    break
        snap.extend(self.dma_last.values())
        for e in ENGINES:
            self.pending[e] = list(snap)

    def emit(self, final_waits=()):
        nc = self.nc
        for op in self.all_ops:
            for d in op.deps:
                if not d.is_dma:
                    d.marked = True
        for e in ENGINES:
            c = 0
            for op in self.ops[e]:
                if op.marked and not op.is_dma:
                    c += 1
                    op.count = c
        if os.environ.get("MK_VERBOSE"):
            print("sem counts", {e: max([o.count for o in self.ops[e]] + [0]) for e in ENGINES}, {e: len(self.ops[e]) for e in ENGINES}, "dma", max(self.dma_counts.values()))
        with ExitStack() as st:
            esem = {e: st.enter_context(nc.semaphore("p_" + e)) for e in ENGINES}
            dsem = {k: st.enter_context(nc.semaphore("d_" + str(k))) for k in self.dma_counts}
            block = st.enter_context(nc.Block())

            def run(engname):
                def body(eng):
                    waited = {}
                    for op in self.ops[engname]:
                        need = {}
                        for d in op.deps:
                            if d.is_dma:
                                k = ("d", d.dma_key)
                                v = d.dma_count
                            else:
                                k = ("e", d.eng)
                                v = d.count
                            if v > need.get(k, 0):
                                need[k] = v
                        for k, v in need.items():
                            if waited.get(k, 0) >= v:
                                continue
                            waited[k] = v
                            eng.wait_ge(dsem[k[1]] if k[0] == "d" else esem[k[1]], v)
                        ins = op.fn(eng)
                        if op.is_dma:
                            ins.then_inc(dsem[op.dma_key], 16)
                        elif op.marked:
                            ins.then_inc(esem[engname], 1)
                    if engname == "sync":
                        for k in final_waits:
                            eng.wait_ge(dsem[k], self.dma_counts[k])
                return body

            block.tensor(run("tensor"))
            block.vector(run("vector"))
            block.scalar(run("scalar"))
            block.gpsimd(run("gpsimd"))
            block.sync(run("sync"))


class Arena:
    def __init__(self, ap, nbytes):
        self.ap = ap
        self.nbytes = nbytes
        self.off = 0

    def alloc(self, free_shape, dtype):
        n = 1
        for s in free_shape:
            n *= s
        size = n * (4 if dtype == F32 else 2)
        self.off = (self.off + 63) // 64 * 64
        assert self.off + size <= self.nbytes, ("arena overflow", self.off, size, self.nbytes)
        sl = self.ap[:, self.off // 2:(self.off + size) // 2]
        self.off += size
        v = sl.bitcast(F32) if dtype == F32 else sl
        if len(free_shape) == 2:
            v = v.rearrange("p (a b) -> p a b", a=free_shape[0])
        elif len(free_shape) == 3:
            v = v.rearrange("p (a b c) -> p a b c", a=free_shape[0], b=free_shape[1])
        elif len(free_shape) == 4:
            v = v.rearrange("p (a b c d) -> p a b c d", a=free_shape[0], b=free_shape[1], c=free_shape[2])
        return v


class _Stop(Exception):
    pass


def build_program(debug=False, stop=""):
    nc = bass.Bass("TRN2", target_bir_lowering=False)

    def din(name, shape):
        return nc.dram_tensor(name, list(shape), F32, kind="ExternalInput").ap()

    xT = din("xT", [1024, 8192])
    xTo = din("xTo", [1024, 2048])
    xo = din("xo", [2048, 1024])
    w_in = din("w_in", [1024, 6680])
    cw1k = din("cw1k", [4096, 128]); cw2k = din("cw2k", [128, 128]); cpeTk = din("cpeTk", [128, 32])
    cw1v = din("cw1v", [4096, 128]); cw2v = din("cw2v", [128, 128]); cpeTv = din("cpeTv", [128, 32])
    wpg = din("wpg", [1024, 1024]); wpn = din("wpn", [1024, 1024]); wout = din("wout", [1024, 1024])
    wff1 = din("wff1", [1024, 4096]); wff2 = din("wff2", [4096, 1024])
    lngT_d = din("lngT", [128, 8]); lnbT_d = din("lnbT", [128, 8])
    wsT_d = din("wsT", [128, 8, 128]); gbs_d = din("gbs", [1024])
    ln1g_d = din("ln1g", [1024]); ln1b_d = din("ln1b", [1024]); ln2g_d = din("ln2g", [1024]); ln2b_d = din("ln2b", [1024])
    csel_d = din("csel", [128, 4, 512]); wsel_d = din("wsel", [128, 8, 512]); maskc_d = din("maskc", [32, 128])
    Lm_d = din("Lm", [32, 4, 128]); ALt_d = din("ALt", [3, 1, 128]); alr_d = din("alr", [16, 3, 4, 2, 512]); alr2_d = din("alr2", [16, 2, 3, 2, 512])
    selA_d = din("selA", [16, 128, 128]); selB_d = din("selB", [16, 128, 128])
    E64_d = din("E64", [128, 32, 128]); vca_d = din("vca", [128, 4, 2, 258]); ident_d = din("identc", [128, 128])
    out_d = nc.dram_tensor("out", [2048, 1024], F32, kind="ExternalOutput").ap()
    ynsa_d = nc.dram_tensor("ynsa_scr", [128, 16, 8, 128], BF16).ap()
    h_d = nc.dram_tensor("h_scr", [2048, 1024], F32).ap()
    dbg_d = None
    if debug:
        dbg_d = nc.dram_tensor("dbg", [2048, 1024], F32, kind="ExternalOutput").ap()

    xTv = xT.rearrange("(c p) t -> p c t", p=128)
    xTov = xTo.rearrange("(c p) t -> p c t", p=128)
    w_inv = w_in.rearrange("(c p) n -> p c n", p=128)

    ARENA_BYTES = 204 * 1024
    with ExitStack() as st:
        arena_t = st.enter_context(nc.sbuf_tensor("arena", [128, ARENA_BYTES // 2], BF16))
        banks = [st.enter_context(nc.psum_tensor("bk%d" % k, [128, 512], F32)) for k in range(8)]
        AR = Arena(arena_t[:], ARENA_BYTES)
        P = Prog(nc)
        bk = lambda k: "bk%d" % k
        cnt = {"ev": 0}

        def mm(out, lhsT, rhs, start, stop, reads, writes):
            P.op("tensor", lambda e: e.matmul(out, lhsT=lhsT, rhs=rhs, start=start, stop=stop), reads, writes)

        def tr(out, in_, reads, writes):
            P.op("tensor", lambda e: e.transpose(out=out, in_=in_, identity=ident[:]), list(reads) + ["ident"], writes)

        def act(out, in_, func, reads, writes, bias=None, scale=None):
            kw = {}
            if bias is not None:
                kw["bias"] = bias
            if scale is not None:
                kw["scale"] = scale
            P.op("scalar", lambda e: e.activation(out=out, in_=in_, func=func, **kw), reads, writes)

        def cp(eng, out, in_, reads, writes):
            if eng == "scalar":
                P.op("scalar", lambda e: e.copy(out=out, in_=in_), reads, writes)
            else:
                P.op(eng, lambda e: e.tensor_copy(out=out, in_=in_), reads, writes)

        def evac(out, in_, reads, writes):
            cnt["ev"] += 1
            cp("scalar" if cnt["ev"] % 2 else "vector", out, in_, reads, writes)

        def ts(eng, out, in0, s1, s2, op0, op1, reads, writes):
            if op1 is None:
                P.op(eng, lambda e: e.tensor_scalar(out=out, in0=in0, scalar1=s1, scalar2=None, op0=op0), reads, writes)
            else:
                P.op(eng, lambda e: e.tensor_scalar(out=out, in0=in0, scalar1=s1, scalar2=s2, op0=op0, op1=op1), reads, writes)

        def tt(eng, out, in0, in1, op, reads, writes):
            P.op(eng, lambda e: e.tensor_tensor(out=out, in0=in0, in1=in1, op=op), reads, writes)

        def stt(out, in0, scalar, in1, op0, op1, reads, writes):
            P.op("vector", lambda e: e.scalar_tensor_tensor(out=out, in0=in0, scalar=scalar, in1=in1, op0=op0, op1=op1), reads, writes)

        def vop(eng, name, reads, writes, **kw):
            P.op(eng, lambda e: getattr(e, name)(**kw), reads, writes)

        def castdma(out, in_, writes, key):
            last = out.shape[-1]
            if len(out.shape) >= 3 and last > 1024:
                for c0 in range(0, last, 1024):
                    castdma1(out[..., c0:c0 + 1024], in_[..., c0:c0 + 1024], writes, key)
            else:
                castdma1(out, in_, writes, key)

        def castdma1(out, in_, writes, key):
            P.dma("gpsimd", lambda e: e.dma_start(out=out, in_=in_, max_dma_last_dim=4096), (), writes, key=key)

        def ldma(out, in_, writes, key, reads=()):
            P.dma("sync", lambda e: e.dma_start(out=out, in_=in_), reads, writes, key=key)

        ident = AR.alloc([128], BF16)
        castdma(ident[:], ident_d, ["ident"], "ident")
        k_cmpT = AR.alloc([2, 512], BF16)
        vca = AR.alloc([4, 2, 258], BF16)
        castdma(vca[:], vca_d, ["vca"], "vca")
        persist_mark = AR.off

        try:
            Wc = AR.alloc([8, 512], BF16)
            w1k = AR.alloc([32, 128], BF16); w1v = AR.alloc([32, 128], BF16)
            w2k = AR.alloc([128], BF16); w2v = AR.alloc([128], BF16)
            peTk = AR.alloc([32], BF16); peTv = AR.alloc([32], BF16)
            biasK = AR.alloc([1], F32); biasV = AR.alloc([1], F32)
            kc_all = AR.alloc([2, 8208], BF16); vc_all = AR.alloc([2, 8208], BF16)
            xg = AR.alloc([8, 512], BF16)
            hT = AR.alloc([2, 128], BF16)

            castdma(Wc[:], w_inv[:, :, 3072:3584], ["Wc"], "Wc")
            castdma(w1k[:], cw1k.rearrange("(p d) f -> d p f", d=128), ["w1k"], "w1k")
            castdma(w1v[:], cw1v.rearrange("(p d) f -> d p f", d=128), ["w1v"], "w1v")
            castdma(w2k[:], cw2k, ["w2k"], "w2k")
            castdma(w2v[:], cw2v, ["w2v"], "w2v")
            castdma(peTk[:], cpeTk, ["peTk"], "peTk")
            castdma(peTv[:], cpeTv, ["peTv"], "peTv")
            P.op("vector", lambda e: e.memset(kc_all[:, :, 0:16], 0.0), (), ["kc_all"])
            P.op("vector", lambda e: e.memset(vc_all[:, :, 0:16], 0.0), (), ["vc_all"])
            for (w1, peT, bias_t, nm, b) in ((w1k, peTk, biasK, "k", 6), (w1v, peTv, biasV, "v", 7)):
                for p in range(32):
                    mm(banks[b][:, 0:1], w1[:, p, :], peT[:, p:p + 1], p == 0, p == 31, ["w1" + nm, "peT" + nm], [bk(b)])
                cp("vector", bias_t[:], banks[b][:, 0:1], [bk(b)], ["bias" + nm])

            xg2 = AR.alloc([8, 512], BF16)
            for i in range(16):
                xg_ = (xg, xg2)[i % 2]
                xk = "xg%d" % (i % 2)
                castdma(xg_[:], xTv[:, :, 512 * i:512 * i + 512], [xk], xk)
                for cg in range(4):
                    b = cg
                    for dc in range(8):
                        mm(banks[b][:, :], Wc[:, dc, 128 * cg:128 * cg + 128], xg_[:, dc, :], dc == 0, dc == 7, ["Wc", xk], [bk(b)])
                    dst = (kc_all if cg < 2 else vc_all)
                    evac(dst[:, cg % 2, 16 + 512 * i:16 + 512 * i + 512], banks[b][:, :], [bk(b)], ["kc_all" if cg < 2 else "vc_all"])
            for t in range(4):
                for isk in (True, False):
                    w1, w2, raw, bias_t, nm = (w1k, w2k, kc_all, biasK, "k") if isk else (w1v, w2v, vc_all, biasV, "v")
                    b = 4 if isk else 5
                    rawkey = "kc_all" if isk else "vc_all"
                    for p in range(32):
                        lo = p + 2048 * t
                        mm(banks[b][:, 0:256], w1[:, p, :], raw[:, :, lo:lo + 16 * 127 + 1:16], p == 0, p == 31, ["w1" + nm, rawkey], [bk(b)])
                    act(hT[:].rearrange("p a b -> p (a b)"), banks[b][:, 0:256], AF.Gelu_apprx_tanh, [bk(b), "bias" + nm], ["hT"], bias=bias_t[:, 0:1])
                    if isk:
                        mm(banks[6][:, 0:256], w2[:, :], hT[:].rearrange("p a b -> p (a b)"), True, True, ["w2k", "hT"], [bk(6)])
                        evac(k_cmpT[:, :, 128 * t:128 * t + 128], banks[6][:, 0:256].rearrange("p (a b) -> p a b", a=2), [bk(6)], ["k_cmpT"])
                    else:
                        for hk in range(2):
                            mm(banks[7][:, 128 * hk:128 * hk + 128], hT[:, hk, :], w2[:, :], True, True, ["w2v", "hT"], [bk(7)])
                        evac(vca[:, t, :, 0:128], banks[7][:, 0:256].rearrange("p (a b) -> p a b", a=2), [bk(7)], ["vca"])

            vop0 = vca[0:1, 0, :, 0:128]
            P.op("vector", lambda e: e.memset(vop0, 0.0), ["vca"], ["vca"])
            if stop == "A0":
                if debug:
                    dtmp = AR.alloc([3, 1024], F32)
                    cp("vector", dtmp[:, 0, :], k_cmpT[:].rearrange("p a b -> p (a b)"), ["k_cmpT"], ["dtmp"])
                    vflat = vca[:].rearrange("p a b c -> p (a b c)")
                    cp("vector", dtmp[:, 1, :], vflat[:, 0:1024], ["vca"], ["dtmp"])
                    cp("vector", dtmp[:, 2, :], vflat[:, 1024:2048], ["vca"], ["dtmp"])
                    for k3 in range(3):
                        ldma(dbg_d[128 * k3:128 * k3 + 128, :], dtmp[:, k3, :], [], "dbg", reads=["dtmp"])
                raise _Stop()
            P.barrier()
            AR.off = persist_mark
            Wkv = AR.alloc([8, 1024], BF16)
            Wq = AR.alloc([8, 1024], BF16)
            Wg = AR.alloc([8, 24], BF16)
            ksT = AR.alloc([2, 8192], BF16)
            vs_aug = AR.alloc([64, 2, 130], BF16)
            kwT = AR.alloc([2, 2, 512], BF16)
            vw_aug = AR.alloc([2, 4, 2, 130], BF16)
            xg = AR.alloc([8, 512], BF16)
            xq = AR.alloc([8, 128], BF16)
            qT = AR.alloc([8, 128], BF16)
            E64 = AR.alloc([32, 128], BF16)
            Rc = [AR.alloc([2, 512], BF16) for _ in range(2)]
            Rw = [AR.alloc([2, 512], BF16) for _ in range(2)]
            csel = AR.alloc([4, 512], BF16)
            wsel = AR.alloc([8, 512], BF16)
            maskc = AR.alloc([128], BF16)
            Lm = AR.alloc([4, 128], BF16)
            ALt = AR.alloc([1, 128], BF16)
            alr = [AR.alloc([4, 2, 512], BF16) for _ in range(2)]
            selA = [AR.alloc([128], F32) for _ in range(2)]
            selB = [AR.alloc([128], F32) for _ in range(2)]
            PT = [AR.alloc([4, 128], BF16) for _ in range(4)]
            y_acc = AR.alloc([8, 128], F32)
            ybf = AR.alloc([8, 128], BF16)
            ynT = AR.alloc([8, 128], BF16)
            gates = AR.alloc([24], F32)
            imp = AR.alloc([128], F32)
            sc = AR.alloc([128], F32)
            sc2 = AR.alloc([128], F32)
            m8a = AR.alloc([8], F32); m8b = AR.alloc([8], F32)
            selb = AR.alloc([128], BF16)
            selbT = AR.alloc([128], BF16)
            zz = AR.alloc([4], F32); rz = AR.alloc([4], F32); coef = AR.alloc([4], F32)
            osb = AR.alloc([4, 258], F32)

            castdma(Wkv[:, :, 0:256], w_inv[:, :, 3584:3840], ["Wkv"], "Wkv")
            castdma(Wkv[:, :, 256:512], w_inv[:, :, 4096:4352], ["Wkv"], "Wkv")
            castdma(Wkv[:, :, 512:768], w_inv[:, :, 3840:4096], ["Wkv"], "Wkv")
            castdma(Wkv[:, :, 768:1024], w_inv[:, :, 4352:4608], ["Wkv"], "Wkv")
            castdma(Wq[:], w_inv[:, :, 2048:3072], ["Wq"], "Wq")
            castdma(Wg[:], w_inv[:, :, 4608:4632], ["Wg"], "Wg")
            castdma(E64[:], E64_d, ["E64"], "E64")
            castdma(csel[:], csel_d, ["csel"], "csel")
            castdma(wsel[:], wsel_d, ["wsel"], "wsel")
            castdma(maskc[0:32, :], maskc_d, ["maskc"], "maskc")
            castdma(Lm[0:32], Lm_d, ["Lm"], "Lm")
            castdma(ALt[0:3], ALt_d, ["ALt"], "ALt")
            P.op("gpsimd", lambda e: e.memset(vs_aug[:], 1.0), (), ["vs_aug"])
            for k2 in range(2):
                vop("gpsimd", "memset", (), ["Rc%d" % k2], ap=Rc[k2][:], constant=0.0)
                vop("gpsimd", "memset", (), ["Rw%d" % k2], ap=Rw[k2][:], constant=0.0)
            P.op("gpsimd", lambda e: e.memset(vw_aug[:], 1.0), (), ["vw_aug"])

            if os.environ.get("MK_SUB", "") == "setup":
                raise _Stop()
            def bcast4(ap2d):
                return ap2d.unsqueeze(1).to_broadcast([ap2d.shape[0], 4, 128])

            def finalize_branch(hk, br, first):
                W = 257 if br == 0 else 129
                for g in range(4):
                    cp("scalar" if g % 2 == 0 else "vector", osb[:, g, 0:W], banks[2 + g][:, 0:W], [bk(2 + g)], ["osb%d" % g])
                ts("vector", zz[:], osb[:, :, 128], 1e-30, None, ALU.max, None, ["osb0", "osb1", "osb2", "osb3"], ["zz"])
                P.op("vector", lambda e: e.reciprocal(out=rz[:], in_=zz[:]), ["zz"], ["rz"])
                gv = gates[:].rearrange("p (h r) -> p h r", r=3)[:, 4 * hk:4 * hk + 4, br]
                tt("vector", coef[:], rz[:], gv, ALU.mult, ["rz", "gates"], ["coef"])
                for g in range(4):
                    h = 4 * hk + g
                    if first:
                        ts("vector", y_acc[:, h, :], osb[:, g, 0:128], coef[:, g:g + 1], None, ALU.mult, None,
                           ["osb%d" % g, "coef"], ["y_acc"])
                    else:
                        stt(y_acc[:, h, :], osb[:, g, 0:128], coef[:, g:g + 1], y_acc[:, h, :], ALU.mult, ALU.add,
                            ["osb%d" % g, "coef", "y_acc"], ["y_acc"])

            sbank = [0, 1, 6]
            scount = {"n": 0}

            def next_sbank():
                scount["n"] += 1
                return sbank[scount["n"] % 3]

            pcount = {"n": 0}

            def next_PT():
                pcount["n"] += 1
                k = pcount["n"] % 4
                return PT[k], "PT%d" % k

            NG = int(os.environ.get("MK_NG", "16"))
            for i in range(NG):
                par = i % 2
                if i == 0:
                    castdma(xg[:], xTv[:, :, 0:512], ["xg"], "xgA")
                    castdma(xq[:], xTov[:, :, 0:128], ["xq"], "xq")
                ldma(selA[par][:], selA_d[i], ["selA%d" % par], "selA%d" % par)
                ldma(selB[par][:], selB_d[i], ["selB%d" % par], "selB%d" % par)
                castdma(alr[par][0:3], alr_d[i], ["alr%d" % par], "alr%d" % par)
                for cg in range(4):
                    b = (0, 1, 6, 7)[cg]
                    for dc in range(8):
                        mm(banks[b][:, :], Wkv[:, dc, 128 * cg:128 * cg + 128], xg[:, dc, :], dc == 0, dc == 7, ["Wkv", "xg"], [bk(b)])
                    if cg < 2:
                        evac(ksT[:, cg, 512 * i:512 * i + 512], banks[b][:, :], [bk(b)], ["ksT"])
                    else:
                        evac(kwT[:, par, cg - 2, :], banks[b][:, :], [bk(b)], ["kwT%d" % par])
                if os.environ.get("MK_SUB", "") == "kproj":
                    raise _Stop()
                for sub in range(4):
                    b = (0, 1, 6, 7)[sub]
                    for dc in range(8):
                        mm(banks[b][:, :], xg[:, dc, 128 * sub:128 * sub + 128], Wkv[:, dc, 512:1024], dc == 0, dc == 7, ["Wkv", "xg"], [bk(b)])
                    mkx = os.environ.get("MK_X", "")
                    if mkx not in ("1", "3"):
                        evac(vs_aug[:, 4 * i + sub, :, 0:128], banks[b][:, 0:256].rearrange("p (a b) -> p a b", a=2), [bk(b)], ["vs_aug"])
                    if mkx not in ("2", "3"):
                        evac(vw_aug[:, par, sub, :, 0:128], banks[b][:, 256:512].rearrange("p (a b) -> p a b", a=2), [bk(b)], ["vw_aug%d" % par])
                if i + 1 < NG:
                    castdma(xg[:], xTv[:, :, 512 * (i + 1):512 * (i + 1) + 512], ["xg"], "xgA")
                if os.environ.get("MK_SUB", "") == "vproj":
                    raise _Stop()
                for hb in range(2):
                    b = 6 + hb
                    for g in range(4):
                        h = 4 * hb + g
                        for dc in range(8):
                            mm(banks[b][:, 128 * g:128 * g + 128], Wq[:, dc, 128 * h:128 * h + 128], xq[:, dc, :], dc == 0, dc == 7, ["Wq", "xq"], [bk(b)])
                    act(qT[:, 4 * hb:4 * hb + 4, :].rearrange("p a b -> p (a b)"), banks[b][:, :], AF.Copy, [bk(b)], ["qT"], scale=QSCALE)
                if os.environ.get("MK_SUB", "") == "qproj":
                    raise _Stop()
                for dc in range(8):
                    mm(banks[0][:, 0:24], xq[:, dc, :], Wg[:, dc, :], dc == 0, dc == 7, ["Wg", "xq"], [bk(0)])
                act(gates[:], banks[0][:, 0:24], AF.Sigmoid, [bk(0)], ["gates"])
                if i + 1 < NG:
                    castdma(xq[:], xTov[:, :, 128 * (i + 1):128 * (i + 1) + 128], ["xq"], "xq")

                if os.environ.get("MK_SUB", "") == "proj":
                    raise _Stop()
                for hk in range(2):
                    q4 = qT[:, 4 * hk:4 * hk + 4, :].rearrange("p a b -> p (a b)")
                    units = []
                    p2 = (2 * i + hk) % 2
                    castdma(Rc[p2][64:67, :, :], alr2_d[i, hk], ["Rc%d" % p2], "Rc%d" % p2)
                    castdma(Rw[p2][64:67, :, :], alr2_d[i, hk], ["Rw%d" % p2], "Rw%d" % p2)

                    nt = i // 4 + 1

                    def cmp_S(t, last, M):
                        sb_ = next_sbank()
                        mm(banks[sb_][0:M, :], k_cmpT[:, hk, 128 * t:128 * t + M], q4, True, False, ["k_cmpT", "qT"], [bk(sb_)])
                        mm(banks[sb_][0:M, :], ALt[0:3, 0, 0:M], alr[par][0:3, t, hk, :], False, not last, ["ALt", "alr%d" % par], [bk(sb_)])
                        if last:
                            mm(banks[sb_][0:M, :], Lm[0:32, i % 4, 0:M], bcast4(maskc[0:32, :]), False, True, ["Lm", "maskc"], [bk(sb_)])
                        pt, ptk = next_PT()
                        act(pt[0:M, :, :].rearrange("p a b -> p (a b)"), banks[sb_][0:M, :], AF.Exp, [bk(sb_)], [ptk])
                        return pt, ptk

                    def cmp_PV(t, last, M, pt, ptk):
                        for g in range(4):
                            mm(banks[2 + g][:, 0:257], pt[0:M, g, :], vca[0:M, t, hk, 0:257], t == 0, last, [ptk, "vca"], [bk(2 + g)])
                        if last:
                            finalize_branch(hk, 0, True)
                            for g in range(4):
                                if g == 0:
                                    ts("vector", imp[:], osb[:, 0, 129:257], rz[:, 0:1], None, ALU.mult, None, ["osb0", "rz"], ["imp"])
                                else:
                                    stt(imp[:], osb[:, g, 129:257], rz[:, g:g + 1], imp[:], ALU.mult, ALU.add, ["osb%d" % g, "rz", "imp"], ["imp"])
                            tt("vector", sc[:], imp[:], selA[par][:], ALU.mult, ["imp", "selA%d" % par], ["sc"])
                            tt("vector", sc[:], sc[:], selB[par][:], ALU.add, ["sc", "selB%d" % par], ["sc"])
                            P.op("vector", lambda e: e.max(out=m8a[:], in_=sc[:]), ["sc"], ["m8a"])
                            P.op("vector", lambda e: e.match_replace(out=sc2[:], in_to_replace=m8a[:], in_values=sc[:], imm_value=-1e30), ["sc", "m8a"], ["sc2"])
                            P.op("vector", lambda e: e.max(out=m8b[:], in_=sc2[:]), ["sc2"], ["m8b"])
                            ts("vector", selb[:], sc[:], m8b[:, 7:8], NEG, ALU.is_lt, ALU.mult, ["sc", "m8b"], ["selb"])
                            tb = banks[7][:].bitcast(BF16)
                            tr(tb[0:64, 0:128], selb[:, 0:64], ["selb"], [bk(7)])
                            tr(tb[0:64, 128:256], selb[:, 64:128], ["selb"], [bk(7)])
                            for a_ in range(2):
                                cp("vector", Rc[p2][0:64, a_, :].rearrange("p (g q) -> p g q", g=4),
                                   tb[0:64, 128 * a_:128 * a_ + 128].unsqueeze(1).to_broadcast([64, 4, 128]), [bk(7)], ["Rc%d" % p2])

                    for t in range(nt):
                        last = (t == nt - 1)
                        M = 32 * (i % 4 + 1) if last else 128
                        units.append((lambda t=t, last=last, M=M: cmp_S(t, last, M),
                                      lambda pt, ptk, t=t, last=last, M=M: cmp_PV(t, last, M, pt, ptk)))

                    r8s = list(range(8)) if i > 0 else list(range(4, 8))

                    def win_S(r8):
                        wpar = (i - 1) % 2 if r8 < 4 else par
                        sub = r8 % 4
                        kt = 4 * (i - 1) + r8
                        sb_ = next_sbank()
                        mm(banks[sb_][:, :], kwT[:, wpar, hk, 128 * sub:128 * sub + 128], q4, True, False, ["kwT%d" % wpar, "qT"], [bk(sb_)])
                        mm(banks[sb_][:, :], E64[:, kt % 32, :], Rw[p2][:, kt // 32, :], False, False, ["E64", "Rw%d" % p2], [bk(sb_)])
                        mm(banks[sb_][:, :], ident[:], wsel[:, r8, :], False, True, ["ident", "wsel"], [bk(sb_)])
                        pt, ptk = next_PT()
                        act(pt[:].rearrange("p a b -> p (a b)"), banks[sb_][:, :], AF.Exp, [bk(sb_)], [ptk])
                        return pt, ptk

                    def win_PV(r8, first, last, pt, ptk):
                        wpar = (i - 1) % 2 if r8 < 4 else par
                        sub = r8 % 4
                        for g in range(4):
                            mm(banks[2 + g][:, 0:129], pt[:, g, :], vw_aug[:, wpar, sub, hk, 0:129], first, last,
                               [ptk, "vw_aug%d" % wpar], [bk(2 + g)])
                        if last:
                            finalize_branch(hk, 2, False)

                    for idx, r8 in enumerate(r8s):
                        units.append((lambda r8=r8: win_S(r8),
                                      lambda pt, ptk, r8=r8, f=(idx == 0), l=(idx == len(r8s) - 1): win_PV(r8, f, l, pt, ptk)))

                    nk = 4 * i + 4

                    def sel_S(kt):
                        diag = kt >= 4 * i
                        sb_ = next_sbank()
                        mm(banks[sb_][:, :], ksT[:, hk, 128 * kt:128 * kt + 128], q4, True, False, ["ksT", "qT"], [bk(sb_)])
                        mm(banks[sb_][:, :], E64[:, kt % 32, :], Rc[p2][:, kt // 32, :], False, not diag, ["E64", "Rc%d" % p2], [bk(sb_)])
                        if diag:
                            mm(banks[sb_][:, :], ident[:], csel[:, kt - 4 * i, :], False, True, ["ident", "csel"], [bk(sb_)])
                        pt, ptk = next_PT()
                        act(pt[:].rearrange("p a b -> p (a b)"), banks[sb_][:, :], AF.Exp, [bk(sb_)], [ptk])
                        return pt, ptk

                    def sel_PV(kt, pt, ptk):
                        for g in range(4):
                            mm(banks[2 + g][:, 0:129], pt[:, g, :], vs_aug[:, kt, hk, 0:129], kt == 0, kt == nk - 1, [ptk, "vs_aug"], [bk(2 + g)])
                        if kt == nk - 1:
                            finalize_branch(hk, 1, False)

                    for kt in range(nk):
                        units.append((lambda kt=kt: sel_S(kt), lambda pt, ptk, kt=kt: sel_PV(kt, pt, ptk)))

                    LA = 2
                    pend = []
                    for (fS, fPV) in units:
                        pend.append((fPV, fS()))
                        if len(pend) > LA:
                            f_, r_ = pend.pop(0)
                            f_(*r_)
                    for f_, r_ in pend:
                        f_(*r_)
                if os.environ.get("MK_SUB", "") == "win":
                    raise _Stop()
                if debug:
                    ldma(dbg_d[128 * i:128 * i + 128, :], y_acc[:].rearrange("p a b -> p (a b)"), [], "dbg", reads=["y_acc"])
                cp("scalar", ybf[:], y_acc[:], ["y_acc"], ["ybf"])
                tb = banks[7][:].bitcast(BF16)
                for c in range(8):
                    tr(tb[:, 128 * c:128 * c + 128], ybf[:, c, :], ["ybf"], [bk(7)])
                cp("vector", ynT[:].rearrange("p a b -> p (a b)"), tb[:, :], [bk(7)], ["ynT"])
                ldma(ynsa_d[:, i, :, :], ynT[:], ["ynsa_d"], "ynsa_w", reads=["ynT"])

            if stop == "A":
                raise _Stop()
            P.barrier()
            AR.off = persist_mark
            ygT_all = AR.alloc([8, 2048], BF16)
            b1a_mark = AR.off
            Wu = AR.alloc([8, 1024], BF16)
            Wv = AR.alloc([8, 1024], BF16)
            xt = AR.alloc([8, 512], BF16)
            uT = AR.alloc([8, 512], BF16)
            vg = [AR.alloc([1024], F32) for _ in range(2)]
            vn = [AR.alloc([1024], BF16) for _ in range(2)]
            WcT = AR.alloc([8, 128], BF16)
            wsf = AR.alloc([8, 128], F32)
            Badd = AR.alloc([8, 128], F32)
            bsrep = AR.alloc([8, 128], F32)
            lngT = AR.alloc([8], F32); lnbT = AR.alloc([8], F32)
            ones = AR.alloc([128], BF16)
            stat = AR.alloc([2, 6], F32); mv = AR.alloc([2], F32); rstd = AR.alloc([1], F32)
            t1 = [AR.alloc([128], F32) for _ in range(2)]

            castdma(Wu[:], w_inv[:, :, 0:1024], ["Wu"], "Wu")
            castdma(Wv[:], w_inv[:, :, 1024:2048], ["Wv"], "Wv")
            ldma(wsf[:], wsT_d, ["wsf"], "wsf")
            ldma(bsrep[:].rearrange("p a b -> p (a b)"), gbs_d.partition_broadcast(128), ["bsrep"], "bsrep")
            ldma(lngT[:], lngT_d, ["lngT"], "lngT")
            ldma(lnbT[:], lnbT_d, ["lnbT"], "lnbT")
            P.op("vector", lambda e: e.memset(ones[:], 1.0), (), ["ones"])
            P.op("gpsimd", lambda e: e.affine_select(out=wsf[:], in_=wsf[:], pattern=[[0, 8], [1, 128]], compare_op=ALU.is_ge,
                                                     fill=0.0, base=0, channel_multiplier=-1), ["wsf"], ["wsf"])
            cp("vector", WcT[:], wsf[:], ["wsf"], ["WcT"])
            for g in range(8):
                b = g % 2
                mm(banks[b][:, 0:128], ones[:], WcT[:, g, :], True, True, ["ones", "WcT"], [bk(b)])
                stt(Badd[:, g, :], banks[b][:, 0:128], lnbT[:, g:g + 1], bsrep[:, g, :], ALU.mult, ALU.add,
                    [bk(b), "lnbT", "bsrep"], ["Badd"])

            def layer_norm_stats(src, srckey, stat, mv, rstd):
                vop("vector", "bn_stats", [srckey], ["stat"], out=stat[:, 0, :], in_=src[:, 0:512])
                vop("vector", "bn_stats", [srckey, "stat"], ["stat"], out=stat[:, 1, :], in_=src[:, 512:1024])
                vop("vector", "bn_aggr", ["stat"], ["mv"], out=mv[:], in_=stat[:].rearrange("p a b -> p (a b)"))
                act(rstd[:], mv[:, 1:2], AF.Sqrt, ["mv"], ["rstd"], bias=LN_EPS, scale=1.0)
                vop("vector", "reciprocal", ["rstd"], ["rstd"], out=rstd[:], in_=rstd[:])

            xtB = AR.alloc([8, 512], BF16)
            xtA = xt
            uTB = AR.alloc([8, 512], BF16)
            uTA = uT
            statB = AR.alloc([2, 6], F32); mvB = AR.alloc([2], F32); rstdB = AR.alloc([1], F32)

            def b1a_front(T, sub):
                xt_ = (xtA, xtB)[T % 2]
                xtk = "xt%d" % (T % 2)
                uT_ = (uTA, uTB)[T % 2]
                uk = "uT%d" % (T % 2)
                if sub == 0:
                    castdma(xt_[:], xTov[:, :, 512 * T:512 * T + 512], [xtk], xtk)
                    for cc in range(8):
                        b = cc % 4
                        for dc in range(8):
                            mm(banks[b][:, :], Wu[:, dc, 128 * cc:128 * cc + 128], xt_[:, dc, :], dc == 0, dc == 7, ["Wu", xtk], [bk(b)])
                        act(uT_[:, cc, :], banks[b][:, :], AF.Gelu_apprx_tanh, [bk(b)], [uk])
                v_ = vg[sub % 2]; vk = "vg%d" % (sub % 2)
                n_ = vn[sub % 2]; nk_ = "vn%d" % (sub % 2)
                st_, mv_, rs_ = ((stat, mv, rstd), (statB, mvB, rstdB))[sub % 2]
                sfx = "_%d" % (sub % 2)
                for half in range(2):
                    b = 4 + half
                    for dc in range(8):
                        mm(banks[b][:, :], xt_[:, dc, 128 * sub:128 * sub + 128], Wv[:, dc, 512 * half:512 * half + 512], dc == 0, dc == 7,
                           ["Wv", xtk], [bk(b)])
                    act(v_[:, 512 * half:512 * half + 512], banks[b][:, :], AF.Gelu_apprx_tanh, [bk(b)], [vk])
                vop("vector", "bn_stats", [vk], ["stat" + sfx], out=st_[:, 0, :], in_=v_[:, 0:512])
                vop("vector", "bn_stats", [vk, "stat" + sfx], ["stat" + sfx], out=st_[:, 1, :], in_=v_[:, 512:1024])
                vop("vector", "bn_aggr", ["stat" + sfx], ["mv" + sfx], out=mv_[:], in_=st_[:].rearrange("p a b -> p (a b)"))
                act(rs_[:], mv_[:, 1:2], AF.Sqrt, ["mv" + sfx], ["rstd" + sfx], bias=LN_EPS, scale=1.0)
                vop("vector", "reciprocal", ["rstd" + sfx], ["rstd" + sfx], out=rs_[:], in_=rs_[:])
                ts("vector", n_[:], v_[:], mv_[:, 0:1], rs_[:, 0:1], ALU.subtract, ALU.mult, [vk, "mv" + sfx, "rstd" + sfx], [nk_])

            def b1a_back(T, sub):
                uT_ = (uTA, uTB)[T % 2]
                uk = "uT%d" % (T % 2)
                n_ = vn[sub % 2]; nk_ = "vn%d" % (sub % 2)
                for g in range(8):
                    b = 6 + (g // 4) % 2
                    mm(banks[b][:, 128 * (g % 4):128 * (g % 4) + 128], n_[:, 128 * g:128 * g + 128], WcT[:, g, :], True, True,
                       [nk_, "WcT"], [bk(b)])
                    if g % 4 == 3:
                        for g2 in range(g - 3, g + 1):
                            tk = t1[g2 % 2]; tkk = "t1%d" % (g2 % 2)
                            stt(tk[:], banks[b][:, 128 * (g2 % 4):128 * (g2 % 4) + 128], lngT[:, g2:g2 + 1], Badd[:, g2, :], ALU.mult, ALU.add,
                                [bk(b), "lngT", "Badd"], [tkk])
                            tt("gpsimd", ygT_all[:, g2, 512 * T + 128 * sub:512 * T + 128 * sub + 128], tk[:], uT_[:, g2, 128 * sub:128 * sub + 128],
                               ALU.mult, [tkk, uk], ["ygT_all"])

            seq = [(T, sub) for T in range(4) for sub in range(4)]
            for k_, (T, sub) in enumerate(seq):
                b1a_front(T, sub)
                if k_ >= 1:
                    b1a_back(*seq[k_ - 1])
            b1a_back(*seq[-1])

            if stop == "B1a":
                raise _Stop()
            P.barrier()
            AR.off = b1a_mark
            Wm = AR.alloc([8, 2048], BF16)
            Wpg = AR.alloc([8, 1024], BF16); Wpn = AR.alloc([8, 1024], BF16); Wo = AR.alloc([8, 1024], BF16)
            xt = AR.alloc([8, 512], BF16)
            ynt = AR.alloc([8, 4, 128], BF16)
            sg = [AR.alloc([512], F32) for _ in range(2)]
            m01 = [AR.alloc([512], F32) for _ in range(2)]
            mrgT = AR.alloc([8, 512], BF16)
            xot = AR.alloc([1024], F32)
            r1 = AR.alloc([1024], F32)
            lg = AR.alloc([1024], F32); lb = AR.alloc([1024], F32)
            stat = AR.alloc([2, 6], F32); mv = AR.alloc([2], F32); rstd = AR.alloc([1], F32)

            castdma(Wpg[:], wpg.rearrange("(c p) n -> p c n", p=128), ["Wpg"], "Wpg")
            castdma(Wpn[:], wpn.rearrange("(c p) n -> p c n", p=128), ["Wpn"], "Wpn")
            castdma(Wm[:], w_inv[:, :, 4632:6680], ["Wm"], "Wm")
            castdma(Wo[:], wout.rearrange("(c p) n -> p c n", p=128), ["Wo"], "Wo")
            ldma(lg[:], ln1g_d.partition_broadcast(128), ["lg"], "lg")
            ldma(lb[:], ln1b_d.partition_broadcast(128), ["lb"], "lb")

            def ln_tail(src, srckey, dst_dram, tag, stat, mv, rstd, lg, lb):
                layer_norm_stats(src, srckey, stat, mv, rstd)
                ts("vector", src[:], src[:], mv[:, 0:1], rstd[:, 0:1], ALU.subtract, ALU.mult, [srckey, "mv", "rstd"], [srckey])
                tt("gpsimd", src[:], src[:], lg[:], ALU.mult, [srckey, "lg"], [srckey])
                tt("gpsimd", src[:], src[:], lb[:], ALU.add, [srckey, "lb"], [srckey])
                ldma(dst_dram, src[:], [tag + "_dram"], tag, reads=[srckey])

            xtB = AR.alloc([8, 512], BF16)
            xtA = xt
            yntB = AR.alloc([8, 4, 128], BF16)
            yntA = ynt
            r1B = AR.alloc([1024], F32)
            r1A = r1
            for T in range(4):
                xt = (xtA, xtB)[T % 2]
                xtk = "xtb%d" % (T % 2)
                ynt = (yntA, yntB)[T % 2]
                yntk = "ynt%d" % (T % 2)
                castdma(xt[:], xTov[:, :, 512 * T:512 * T + 512], [xtk], xtk)
                for il in range(4):
                    ldma(ynt[:, :, il, :], ynsa_d[:, 4 * T + il, :, :], [yntk], yntk, reads=["ynsa_d"])
                for Dc in range(8):
                    s4 = 4 * (Dc % 2)
                    for dc in range(8):
                        mm(banks[s4 + 0][:, :], Wpg[:, dc, 128 * Dc:128 * Dc + 128], ygT_all[:, dc, 512 * T:512 * T + 512], dc == 0, dc == 7,
                           ["Wpg", "ygT_all"], [bk(s4 + 0)])
                    for dc in range(8):
                        mm(banks[s4 + 1][:, :], Wpn[:, dc, 128 * Dc:128 * Dc + 128], ynt[:, dc, :, :].rearrange("p a b -> p (a b)"), dc == 0, dc == 7,
                           ["Wpn", yntk], [bk(s4 + 1)])
                    for br in range(2):
                        for dc in range(8):
                            mm(banks[s4 + 2 + br][:, :], Wm[:, dc, 1024 * br + 128 * Dc:1024 * br + 128 * Dc + 128], xt[:, dc, :], dc == 0, dc == 7,
                               ["Wm", xtk], [bk(s4 + 2 + br)])
                        act(sg[br][:], banks[s4 + 2 + br][:, :], AF.Sigmoid, [bk(s4 + 2 + br)], ["sg%d" % br])
                    tt("vector", m01[0][:], sg[0][:], banks[s4 + 0][:, :], ALU.mult, ["sg0", bk(s4 + 0)], ["m0"])
                    tt("vector", m01[1][:], sg[1][:], banks[s4 + 1][:, :], ALU.mult, ["sg1", bk(s4 + 1)], ["m1"])
                    tt("gpsimd", mrgT[:, Dc, :], m01[0][:], m01[1][:], ALU.add, ["m0", "m1"], ["mrgT"])
                for sub in range(4):
                    o0 = 512 * T + 128 * sub
                    r1 = (r1A, r1B)[sub % 2]
                    r1k = "r1%d" % (sub % 2)
                    ldma(xot[:], xo[o0:o0 + 128, :], ["xot"], "xot")
                    for half in range(2):
                        b = half
                        for Dc in range(8):
                            mm(banks[b][:, :], mrgT[:, Dc, 128 * sub:128 * sub + 128], Wo[:, Dc, 512 * half:512 * half + 512], Dc == 0, Dc == 7,
                               ["mrgT", "Wo"], [bk(b)])
                        stt(r1[:, 512 * half:512 * half + 512], xot[:, 512 * half:512 * half + 512], ALPHA, banks[b][:, :], ALU.mult, ALU.add,
                            ["xot", bk(b)], [r1k])
                    ln_tail(r1, r1k, h_d[o0:o0 + 128, :], "h_w", stat, mv, rstd, lg, lb)

            if stop == "B1b":
                raise _Stop()
            P.barrier()
            AR.off = persist_mark
            W1 = AR.alloc([8, 4096], BF16)
            W2 = AR.alloc([32, 1024], BF16)
            hin = [AR.alloc([1024], F32) for _ in range(2)]
            hbf = AR.alloc([1024], BF16)
            hT2 = AR.alloc([8, 256], BF16)
            aT = AR.alloc([32, 256], BF16)
            r2 = AR.alloc([1024], F32)
            r2B = AR.alloc([1024], F32)
            r2A = r2
            rta = AR.alloc([256], F32); rtv = AR.alloc([256], F32)
            lg = AR.alloc([1024], F32); lb = AR.alloc([1024], F32)
            stat = AR.alloc([2, 6], F32); mv = AR.alloc([2], F32); rstd = AR.alloc([1], F32)
            w1v_ = wff1.rearrange("(c p) n -> p c n", p=128)
            castdma(W1[:, :, 0:2048], w1v_[:, :, 0:2048], ["W1"], "W1")
            castdma(W1[:, :, 2048:4096], w1v_[:, :, 2048:4096], ["W1"], "W1")
            w2v_ = wff2.rearrange("(c p) n -> p c n", p=128)
            for q in range(4):
                castdma(W2[:, 8 * q:8 * q + 8, :], w2v_[:, 8 * q:8 * q + 8, :], ["W2"], "W2")
            ldma(lg[:], ln2g_d.partition_broadcast(128), ["lg"], "lg2")
            ldma(lb[:], ln2b_d.partition_broadcast(128), ["lb"], "lb2")

            for T8 in range(8):
                for s2 in range(2):
                    o0 = 256 * T8 + 128 * s2
                    ldma(hin[s2][:], h_d[o0:o0 + 128, :], ["hin%d" % s2], "hin%d" % s2, reads=["h_w_dram"])
                    cp("scalar", hbf[:], hin[s2][:], ["hin%d" % s2], ["hbf"])
                    tb = banks[6 + s2][:].bitcast(BF16)
                    for c in range(8):
                        tr(tb[:, 128 * c:128 * c + 128], hbf[:, 128 * c:128 * c + 128], ["hbf"], [bk(6 + s2)])
                    cp("vector", hT2[:, :, 128 * s2:128 * s2 + 128], tb[:, :].rearrange("p (a b) -> p a b", a=8), [bk(6 + s2)], ["hT2"])
                for fc in range(32):
                    b = fc % 4
                    for dc in range(8):
                        mm(banks[b][:, 0:256], W1[:, dc, 128 * fc:128 * fc + 128], hT2[:, dc, :], dc == 0, dc == 7, ["W1", "hT2"], [bk(b)])
                    if fc % 2 == 0:
                        act(rta[:], banks[b][:, 0:256], AF.Relu, [bk(b)], ["rta"])
                        tt("gpsimd", aT[:, fc, :], rta[:], rta[:], ALU.mult, ["rta"], ["aT"])
                    else:
                        ts("vector", rtv[:], banks[b][:, 0:256], 0.0, None, ALU.max, None, [bk(b)], ["rtv"])
                        tt("vector", aT[:, fc, :], rtv[:], rtv[:], ALU.mult, ["rtv"], ["aT"])
                for s2 in range(2):
                    o0 = 256 * T8 + 128 * s2
                    r2 = (r2A, r2B)[s2]
                    r2k = "r2%d" % s2
                    for half in range(2):
                        b = 4 + half
                        for fc in range(32):
                            mm(banks[b][:, :], aT[:, fc, 128 * s2:128 * s2 + 128], W2[:, fc, 512 * half:512 * half + 512], fc == 0, fc == 31,
                               ["aT", "W2"], [bk(b)])
                        stt(r2[:, 512 * half:512 * half + 512], hin[s2][:, 512 * half:512 * half + 512], ALPHA, banks[b][:, :], ALU.mult, ALU.add,
                            ["hin%d" % s2, bk(b)], [r2k])
                    ln_tail(r2, r2k, out_d[o0:o0 + 128, :], "out_w", stat, mv, rstd, lg, lb)


        except _Stop:
            pass
        fin = list(P.dma_counts.keys())
        P.emit(final_waits=fin)
    return nc


def _core_constants(j):
    f = np.float32
    q = np.arange(128)
    key = np.arange(128)
    slopes = (2.0 ** (-(np.arange(8) + 1.0))).astype(np.float64)
    c = {}
    kp = 128 * np.arange(4)[None, :, None] + key[:, None, None]
    qp = 128 * j + q[None, None, :]
    c["csel"] = np.ascontiguousarray(np.tile(np.where(kp <= qp, 0.0, NEG).astype(f), (1, 1, 4)))
    kp8 = 128 * (np.arange(8)[None, :, None] - 4) + key[:, None, None]
    dist = qp - kp8
    c["wsel"] = np.ascontiguousarray(np.tile(np.where((dist >= 0) & (dist < 512), 0.0, NEG).astype(f), (1, 1, 4)))
    l = np.arange(32)
    c["maskc"] = np.where(16 * l[:, None] + 15 <= 128 * j + q[None, :], 0.0, NEG).astype(f)
    Lm = np.zeros((32, 4, 128), f)
    for r in range(4):
        Lm[l, r, 32 * r + l] = 1.0
    c["Lm"] = Lm
    p = np.arange(128)
    ALt = np.zeros((3, 1, 128), f)
    ALt[0, 0, :] = 16.0 * p
    ALt[1, 0, :] = 1.0
    ALt[2, 0, :] = 1.0
    c["ALt"] = ALt
    alr = np.zeros((16, 3, 4, 2, 4, 128), np.float64)
    for i in range(16):
        for hk in range(2):
            for g_ in range(4):
                sl = slopes[4 * hk + g_]
                for t in range(4):
                    alr[i, 0, t, hk, g_, :] = sl
                    alr[i, 1, t, hk, g_, :] = sl * 64.0 * (32 * t - 8 * i - 2 * j - 1)
                    alr[i, 2, t, hk, g_, :] = -0.5 * sl
    c["alr"] = alr.reshape(16, 3, 4, 2, 512).astype(f)
    alr2 = np.zeros((16, 2, 3, 2, 4, 128), np.float64)
    for i in range(16):
        for hk in range(2):
            for g_ in range(4):
                sl = slopes[4 * hk + g_]
                for a_ in range(2):
                    alr2[i, hk, 0, a_, g_, :] = sl
                    alr2[i, hk, 1, a_, g_, :] = sl
                    alr2[i, hk, 2, a_, g_, :] = sl * 64.0 * (64 * a_ - 8 * i - 2 * j - 1)
    c["alr2"] = alr2.reshape(16, 2, 3, 2, 512).astype(f)
    selA = np.zeros((16, 128, 128), f)
    selB = np.zeros((16, 128, 128), f)
    m = np.arange(128)
    for i in range(16):
        cur = 8 * i + 2 * j + (q >= 64).astype(np.int64)
        lag = cur[:, None] - m[None, :]
        forced = (m[None, :] == 0) | ((lag >= 0) & (lag < 2))
        selA[i] = ((lag >= 0) & (~forced)).astype(f)
        selB[i] = np.where(forced, 1e9, np.where(lag >= 0, 0.0, -1.0)).astype(f)
    c["selA"] = selA
    c["selB"] = selB
    return c


def _shared_constants():
    f = np.float32
    E64 = np.zeros((128, 32, 128), f)
    key = np.arange(128)
    for v in range(32):
        lr = 2 * v + key // 64
        E64[lr, v, key] = 1.0
        E64[64, v, :] = key
        E64[65, v, :] = 128.0 * v
        E64[66, v, :] = 1.0
    vca = np.zeros((128, 4, 2, 258), f)
    vca[:, :, :, 128] = 1.0
    m = np.arange(128)
    for t in range(4):
        n = 128 * t + np.arange(128) - 1
        ov = ((16 * n[:, None] + 31 >= 64 * m[None, :]) & (16 * n[:, None] <= 64 * m[None, :] + 63) & (n[:, None] >= 0)).astype(f)
        vca[:, t, 0, 129:257] = ov
        vca[:, t, 1, 129:257] = ov
    vca[0, 0, :, :] = 0.0
    return {"E64": E64, "vca": vca, "identc": np.eye(128, dtype=f)}


_CACHE = {}


def kernel(**inputs):
    debug = bool(int(os.environ.get("MK_DEBUG", "0")))
    f = np.float32
    x = np.asarray(inputs["x"], f)
    g = lambda k: np.ascontiguousarray(np.asarray(inputs[k], f)[0])
    shared = {
        "w_in": g("w_in"),
        "cw1k": g("cmp_w1_k"), "cw2k": g("cmp_w2_k"), "cpeTk": np.ascontiguousarray(g("cmp_pe_k").T),
        "cw1v": g("cmp_w1_v"), "cw2v": g("cmp_w2_v"), "cpeTv": np.ascontiguousarray(g("cmp_pe_v").T),
        "wpg": g("w_proj_gm"), "wpn": g("w_proj_nsa"), "wout": g("w_out"),
        "wff1": g("w_ff1"), "wff2": g("w_ff2"),
        "lngT": np.ascontiguousarray(g("gm_ln_g").reshape(8, 128).T), "lnbT": np.ascontiguousarray(g("gm_ln_b").reshape(8, 128).T),
        "wsT": np.ascontiguousarray(g("gm_w_s").transpose(2, 0, 1)),
        "gbs": np.ascontiguousarray(g("gm_b_s").reshape(1024)),
        "ln1g": g("ln1_g"), "ln1b": g("ln1_b"), "ln2g": g("ln2_g"), "ln2b": g("ln2_b"),
    }
    shared.update(_shared_constants())
    xTb = [np.ascontiguousarray(x[b].T) for b in range(2)]
    in_maps = []
    idxs = []
    for c in range(8):
        b, j = c // 4, c % 4
        idx = (512 * np.arange(16)[:, None] + 128 * j + np.arange(128)[None, :]).reshape(-1)
        idxs.append((b, idx))
        xo = np.ascontiguousarray(x[b][idx])
        m = dict(shared)
        m["xT"] = xTb[b]
        m["xo"] = xo
        m["xTo"] = np.ascontiguousarray(xo.T)
        m.update(_core_constants(j))
        in_maps.append(m)
    stop = os.environ.get("MK_STOP", "")
    key = ("nc", debug, stop)
    if key not in _CACHE:
        _CACHE[key] = build_program(debug, stop)
    nc = _CACHE[key]
    ncores = int(os.environ.get("MK_CORES", "8"))
    res = run_bass_kernel_spmd(nc, in_maps[:ncores], core_ids=list(range(ncores)))
    out = np.empty((2, 8192, 1024), f)
    out[:] = 0
    for c in range(ncores):
        b, idx = idxs[c]
        out[b, idx] = res.results[c]["out"]
    if debug:
        dbg = np.zeros((2, 8192, 1024), f)
        for c in range(ncores):
            b, idx = idxs[c]
            dbg[b, idx] = res.results[c]["dbg"]
        kernel.dbg = dbg
    return out
```
